# Optimizing a Trainium2 kernel written in Bass

```python
import math
import jax, jax.numpy as jnp
from jax import lax
import numpy as np

D_MODEL = 1024
BATCH = 2
SEQ = 8192
DEPTH = 1

MEM_LEN = 256
EPS = 1e-6
CONV_WIDTH = 4
SSD_EXPAND = 2
SSD_WIDTH = SSD_EXPAND * D_MODEL
SSD_HEAD_DIM = 64
SSD_HEADS = SSD_WIDTH // SSD_HEAD_DIM
SSD_GROUPS = 4
SSD_STATE = 128
SSD_CHUNK = 128
SSD_CONV_CH = SSD_WIDTH + 2 * SSD_GROUPS * SSD_STATE
LRU_WIDTH = 3 * D_MODEL // 2
LRU_BLOCKS = 16
LRU_BLOCK = LRU_WIDTH // LRU_BLOCKS
LRU_C = 8.0
MEM_HEADS = 4
MEM_HEAD_DIM = D_MODEL // MEM_HEADS
N_BRANCH = 3
SPLIT_POINTS = (
    SSD_WIDTH,
    SSD_WIDTH + SSD_CONV_CH,
    SSD_WIDTH + SSD_CONV_CH + SSD_HEADS,
    SSD_WIDTH + SSD_CONV_CH + SSD_HEADS + LRU_WIDTH,
    SSD_WIDTH + SSD_CONV_CH + SSD_HEADS + 2 * LRU_WIDTH,
    SSD_WIDTH + SSD_CONV_CH + SSD_HEADS + 2 * LRU_WIDTH + D_MODEL,
)
IN_WIDTH = SSD_WIDTH + SSD_CONV_CH + SSD_HEADS + 2 * LRU_WIDTH + D_MODEL + N_BRANCH * D_MODEL

kernel_name = "hybrid_ssd_rglru_memxattn_gated_block"


def rms_norm(x, g):
    xf = x.astype(jnp.float32)
    y = xf * lax.rsqrt(jnp.mean(xf * xf, axis=-1, keepdims=True) + EPS)
    return (y * g.astype(jnp.float32)).astype(x.dtype)


def causal_dwconv(x, w, b):
    k = w.shape[0]
    y = lax.conv_general_dilated(
        x, w[:, None, :].astype(x.dtype), window_strides=(1,), padding=[(k - 1, 0)],
        dimension_numbers=('NWC', 'WIO', 'NWC'), feature_group_count=x.shape[-1])
    return y + b


def ssd_scan(x, dt, a_neg, b_in, c_in):
    f32 = jnp.float32
    bsz, s, h, p = x.shape
    g, n = b_in.shape[2], b_in.shape[3]
    k = h // g
    l = SSD_CHUNK
    nc = s // l
    x = x.astype(f32).reshape(bsz, nc, l, g, k, p)
    dt = dt.astype(f32).reshape(bsz, nc, l, g, k)
    bm = b_in.astype(f32).reshape(bsz, nc, l, g, n)
    cm = c_in.astype(f32).reshape(bsz, nc, l, g, n)
    xdt = x * dt[..., None]
    a_cs = jnp.cumsum(dt * a_neg.astype(f32).reshape(g, k), axis=2)
    causal = jnp.tril(jnp.ones((l, l), dtype=bool))
    seg = a_cs[:, :, :, None] - a_cs[:, :, None, :]
    decay = jnp.exp(jnp.where(causal[:, :, None, None], seg, -jnp.inf))
    cb = jnp.einsum('bclgn,bcsgn->bclsg', cm, bm)
    y_diag = jnp.einsum('bclsgk,bcsgkp->bclgkp', decay * cb[..., None], xdt)
    decay_to_end = jnp.exp(a_cs[:, :, -1:] - a_cs)
    states = jnp.einsum('bclgn,bclgkp->bcgkpn', bm, xdt * decay_to_end[..., None])
    chunk_decay = jnp.exp(a_cs[:, :, -1])

    def step(carry, inp):
        st, dec = inp
        return carry * dec[..., None, None] + st, carry

    init = jnp.zeros((bsz, g, k, p, n), f32)
    _, prev = lax.scan(step, init, (jnp.moveaxis(states, 1, 0), jnp.moveaxis(chunk_decay, 1, 0)))
    prev = jnp.moveaxis(prev, 0, 1)
    y_off = jnp.einsum('bclgn,bcgkpn->bclgkp', cm, prev) * jnp.exp(a_cs)[..., None]
    return (y_diag + y_off).reshape(bsz, s, h, p)


def rg_lru(x, w_a, b_a, w_x, b_x, lam):
    f32 = jnp.float32
    bsz, s, w = x.shape
    xb = x.reshape(bsz, s, LRU_BLOCKS, LRU_BLOCK)
    r = jax.nn.sigmoid(jnp.einsum('bsni,nij->bsnj', xb, w_a) + b_a).reshape(bsz, s, w)
    i = jax.nn.sigmoid(jnp.einsum('bsni,nij->bsnj', xb, w_x) + b_x).reshape(bsz, s, w)
    log_a = (-LRU_C * r.astype(f32)) * jax.nn.softplus(-lam.astype(f32))
    a = jnp.exp(log_a)
    mult = jnp.sqrt(-jnp.expm1(2.0 * log_a))
    u = mult * (i * x).astype(f32)

    def combine(left, right):
        a1, b1 = left
        a2, b2 = right
        return a1 * a2, a2 * b1 + b2

    _, hs = lax.associative_scan(combine, (a, u), axis=1)
    return hs.astype(x.dtype)


def memory_attention(q, mem_n, w_kv):
    bsz, s, _ = q.shape
    m = mem_n.shape[1]
    kv = mem_n @ w_kv
    k, v = jnp.split(kv, 2, axis=-1)
    q = q.reshape(bsz, s, MEM_HEADS, MEM_HEAD_DIM)
    k = k.reshape(bsz, m, MEM_HEADS, MEM_HEAD_DIM)
    v = v.reshape(bsz, m, MEM_HEADS, MEM_HEAD_DIM)
    scores = jnp.einsum('bshd,bmhd->bhsm', q, k).astype(jnp.float32) * (MEM_HEAD_DIM ** -0.5)
    probs = jax.nn.softmax(scores, axis=-1).astype(v.dtype)
    return jnp.einsum('bhsm,bmhd->bshd', probs, v).reshape(bsz, s, D_MODEL)


def setup_inputs(seed: int = 0) -> dict:
    key = jax.random.key(seed)
    ks = jax.random.split(key, 24)
    f32 = jnp.float32
    nrm = lambda k, shape, scale: jax.random.normal(k, shape, f32) * scale
    x = jax.random.normal(ks[0], (BATCH, SEQ, D_MODEL), f32)
    mem = jax.random.normal(ks[1], (BATCH, MEM_LEN, D_MODEL), f32)
    norm_g = 1.0 + nrm(ks[2], (DEPTH, D_MODEL), 0.02)
    w_in = nrm(ks[3], (DEPTH, D_MODEL, IN_WIDTH), D_MODEL ** -0.5)
    ssd_conv_w = nrm(ks[4], (DEPTH, CONV_WIDTH, SSD_CONV_CH), CONV_WIDTH ** -0.5)
    ssd_conv_b = nrm(ks[5], (DEPTH, SSD_CONV_CH), 0.02)
    dt0 = jnp.exp(jax.random.uniform(ks[6], (DEPTH, SSD_HEADS), f32, math.log(1e-3), math.log(1e-1)))
    ssd_dt_bias = dt0 + jnp.log(-jnp.expm1(-dt0))
    ssd_a_log = jnp.log(jax.random.uniform(ks[7], (DEPTH, SSD_HEADS), f32, 1.0, 16.0))
    ssd_d = 1.0 + nrm(ks[8], (DEPTH, SSD_HEADS), 0.02)
    ssd_norm_g = 1.0 + nrm(ks[9], (DEPTH, SSD_GROUPS, SSD_WIDTH // SSD_GROUPS), 0.02)
    lru_conv_w = nrm(ks[10], (DEPTH, CONV_WIDTH, LRU_WIDTH), CONV_WIDTH ** -0.5)
    lru_conv_b = nrm(ks[11], (DEPTH, LRU_WIDTH), 0.02)
    lru_w_a = nrm(ks[12], (DEPTH, LRU_BLOCKS, LRU_BLOCK, LRU_BLOCK), LRU_BLOCK ** -0.5)
    lru_b_a = nrm(ks[13], (DEPTH, LRU_BLOCKS, LRU_BLOCK), 0.02)
    lru_w_x = nrm(ks[14], (DEPTH, LRU_BLOCKS, LRU_BLOCK, LRU_BLOCK), LRU_BLOCK ** -0.5)
    lru_b_x = nrm(ks[15], (DEPTH, LRU_BLOCKS, LRU_BLOCK), 0.02)
    a8 = jax.random.uniform(ks[16], (DEPTH, LRU_WIDTH), f32, 0.9, 0.999)
    sig = a8 ** (1.0 / LRU_C)
    lru_lambda = jnp.log(sig) - jnp.log1p(-sig)
    mem_norm_g = 1.0 + nrm(ks[17], (DEPTH, D_MODEL), 0.02)
    w_kv = nrm(ks[18], (DEPTH, D_MODEL, 2 * D_MODEL), D_MODEL ** -0.5)
    w_br_ssd = nrm(ks[19], (DEPTH, SSD_WIDTH, D_MODEL), SSD_WIDTH ** -0.5)
    w_br_lru = nrm(ks[20], (DEPTH, LRU_WIDTH, D_MODEL), LRU_WIDTH ** -0.5)
    w_br_mem = nrm(ks[21], (DEPTH, D_MODEL, D_MODEL), D_MODEL ** -0.5)
    w_out = nrm(ks[22], (DEPTH, D_MODEL, D_MODEL), D_MODEL ** -0.5)
    final_g = 1.0 + nrm(ks[23], (D_MODEL,), 0.02)
    return {"x": x, "mem": mem, "norm_g": norm_g, "w_in": w_in,
            "ssd_conv_w": ssd_conv_w, "ssd_conv_b": ssd_conv_b, "ssd_dt_bias": ssd_dt_bias,
            "ssd_a_log": ssd_a_log, "ssd_d": ssd_d, "ssd_norm_g": ssd_norm_g,
            "lru_conv_w": lru_conv_w, "lru_conv_b": lru_conv_b, "lru_w_a": lru_w_a,
            "lru_b_a": lru_b_a, "lru_w_x": lru_w_x, "lru_b_x": lru_b_x, "lru_lambda": lru_lambda,
            "mem_norm_g": mem_norm_g, "w_kv": w_kv, "w_br_ssd": w_br_ssd, "w_br_lru": w_br_lru,
            "w_br_mem": w_br_mem, "w_out": w_out, "final_g": final_g}


def reference(x, mem, norm_g, w_in, ssd_conv_w, ssd_conv_b, ssd_dt_bias, ssd_a_log, ssd_d,
              ssd_norm_g, lru_conv_w, lru_conv_b, lru_w_a, lru_b_a, lru_w_x, lru_b_x, lru_lambda,
              mem_norm_g, w_kv, w_br_ssd, w_br_lru, w_br_mem, w_out, final_g):
    bsz, s, _ = x.shape
    for l in range(DEPTH):
        h = rms_norm(x, norm_g[l])
        proj = h @ w_in[l]
        z, xbc, dt_raw, lru_gate, lru_x, q, gate_logits = jnp.split(proj, SPLIT_POINTS, axis=-1)

        xbc = jax.nn.silu(causal_dwconv(xbc, ssd_conv_w[l], ssd_conv_b[l]))
        xs, bs, cs = jnp.split(xbc, [SSD_WIDTH, SSD_WIDTH + SSD_GROUPS * SSD_STATE], axis=-1)
        xs = xs.reshape(bsz, s, SSD_HEADS, SSD_HEAD_DIM)
        dt = jax.nn.softplus((dt_raw + ssd_dt_bias[l]).astype(jnp.float32))
        y = ssd_scan(xs, dt, -jnp.exp(ssd_a_log[l].astype(jnp.float32)),
                     bs.reshape(bsz, s, SSD_GROUPS, SSD_STATE), cs.reshape(bsz, s, SSD_GROUPS, SSD_STATE))
        y = (y + xs.astype(jnp.float32) * ssd_d[l][:, None].astype(jnp.float32)).astype(x.dtype)
        y = y.reshape(bsz, s, SSD_WIDTH) * jax.nn.silu(z)
        y_ssd = rms_norm(y.reshape(bsz, s, SSD_GROUPS, SSD_WIDTH // SSD_GROUPS),
                         ssd_norm_g[l]).reshape(bsz, s, SSD_WIDTH)

        xl = causal_dwconv(lru_x, lru_conv_w[l], lru_conv_b[l])
        y_lru = rg_lru(xl, lru_w_a[l], lru_b_a[l], lru_w_x[l], lru_b_x[l], lru_lambda[l]) * jax.nn.silu(lru_gate)

        mem_n = rms_norm(mem, mem_norm_g[l])
        y_mem = memory_attention(q, mem_n, w_kv[l])

        gates = jax.nn.sigmoid(gate_logits).reshape(bsz, s, N_BRANCH, D_MODEL)
        merged = (gates[:, :, 0] * (y_ssd @ w_br_ssd[l])
                  + gates[:, :, 1] * (y_lru @ w_br_lru[l])
                  + gates[:, :, 2] * (y_mem @ w_br_mem[l]))
        x = x + merged @ w_out[l]
    return rms_norm(x, final_g)
```

```python
import os
import numpy as np
import concourse.bass as bass
import concourse.mybir as mybir
from concourse.bass_utils import run_bass_kernel_spmd
from contextlib import ExitStack

F32 = mybir.dt.float32
BF16 = mybir.dt.bfloat16
AF = mybir.ActivationFunctionType
ALU = mybir.AluOpType
AX = mybir.AxisListType

D = 1024
T = 2048
TH = T + 3
NCH = 16
COL_Z, COL_X, COL_B, COL_C, COL_DT, COL_LG, COL_LX, COL_Q, COL_G = 0, 2048, 4096, 4608, 5120, 5152, 6688, 8224, 9248
IN_W = 12320
NPAY = 2048 + 32 + 12 + 12
EPS = 1e-6
DEBUG = int(os.environ.get("MK_DEBUG", "0"))
STOP = os.environ.get("MK_STOP", "")

PP_NORMG = 0
PP_MEMG = 8
PP_SCONV = 16
PP_LCONV = 136
PP_LBA = 196
PP_LBX = 208
PP_LAM = 220
PP_SNG = 232
PP_DTB = 248
PP_ALOG = 249
PP_N = 250
PB_DTB = 0
PB_ALOG = 32
PB_FG = 64
PB_N = 1088
C_ID = 0
C_U = 128
C_ONE = 256
C_NEG = 384
C_N = 512


class Res:
    __slots__ = ("name", "w", "r")

    def __init__(self, name=""):
        self.name = name
        self.w = None
        self.r = []


class Op:
    __slots__ = ("eng", "fn", "deps", "flag", "val", "is_dma", "slot", "sem")


class Prog:
    ENG = ("pe", "act", "dve", "pool", "sp")
    K = 8

    def __init__(self):
        self.ops = {e: [] for e in self.ENG}
        self.ndma = {e: 0 for e in self.ENG}
        self.fdeps = []
        self.fpend = set()

    def fence(self):
        deps = []
        for e in self.ENG:
            ndm = 0
            got_c = False
            for o in reversed(self.ops[e]):
                if o.is_dma:
                    if ndm < self.K + 2:
                        deps.append(o)
                        ndm += 1
                elif not got_c:
                    deps.append(o)
                    got_c = True
                if got_c and ndm >= self.K + 2:
                    break
        self.fdeps = deps
        self.fpend = set(self.ENG)

    def op(self, eng, fn, reads=(), writes=(), dma=False):
        o = Op()
        o.eng, o.fn, o.is_dma, o.flag, o.val, o.sem = eng, fn, dma, False, 0, None

        def flat(xs):
            out = []
            for x in xs:
                if isinstance(x, (list, tuple)):
                    out.extend(flat(x))
                else:
                    out.append(x)
            return out
        reads, writes = flat(reads), flat(writes)
        deps = []
        for r in reads:
            if r.w is not None:
                deps.append(r.w)
        for w in writes:
            if w.w is not None:
                deps.append(w.w)
            deps.extend(w.r)
        if eng in self.fpend:
            self.fpend.discard(eng)
            deps.extend(self.fdeps)
        dd, seen = [], set()
        for d in deps:
            if id(d) in seen or d is o:
                continue
            seen.add(id(d))
            if (not d.is_dma) and (not dma) and d.eng == "pe" and eng == "pe":
                continue
            dd.append(d)
            if not d.is_dma:
                d.flag = True
        o.deps = dd
        if dma:
            o.slot = self.ndma[eng]
            self.ndma[eng] += 1
        for r in reads:
            if not dma:
                r.r = [x for x in r.r if x.is_dma or x.eng != eng]
            r.r.append(o)
        for w in writes:
            w.w = o
            w.r = []
        self.ops[eng].append(o)
        return o

    def custom(self, eng, fn, sem, reads=(), writes=()):
        o = self.op(eng, fn, reads, writes, dma=True)
        self.ndma[eng] -= 1
        o.slot = -1
        o.sem = sem
        return o

    def dma(self, eng, out, in_, reads=(), writes=()):
        return self.op(eng, lambda e: e.dma_start(out=out, in_=in_), reads, writes, dma=True)

    def emit(self, nc, block, esem, dsem):
        for e in self.ENG:
            n = 0
            for o in self.ops[e]:
                if o.is_dma and o.slot < 0:
                    o.val = 1
                elif o.is_dma:
                    o.sem = dsem[e][o.slot % self.K]
                    o.val = 16 * (o.slot // self.K + 1)
                elif o.flag:
                    n += 1
                    o.val = n
                    o.sem = esem[e]

        def run(ename):
            def body(eng):
                seen = {}

                def wait(sem, val):
                    if seen.get(sem, 0) < val:
                        eng.wait_ge(sem, val)
                        seen[sem] = val

                for o in self.ops[ename]:
                    for d in o.deps:
                        wait(d.sem, d.val)
                    if o.is_dma and o.slot >= self.K:
                        wait(o.sem, o.val - 16)
                    ins = o.fn(eng)
                    if o.is_dma and o.slot < 0:
                        ins.then_inc(o.sem)
                    elif o.is_dma:
                        ins.then_inc(o.sem, 16)
                    elif o.flag:
                        ins.then_inc(o.sem, 1)
            return body

        block.tensor(run("pe"))
        block.scalar(run("act"))
        block.vector(run("dve"))
        block.gpsimd(run("pool"))
        block.sync(run("sp"))


class Ring:
    def __init__(self, tiles):
        self.t = tiles
        self.r = [Res() for _ in tiles]
        self.i = 0

    def next(self):
        k = self.i % len(self.t)
        self.i += 1
        return self.t[k], self.r[k]


def build_nc():
    nc = bass.Bass("TRN2", target_bir_lowering=False)
    P = Prog()

    def din(name, shape, dt=F32):
        return nc.dram_tensor(name, shape, dt, kind="ExternalInput").ap()

    xT_d = din("xT", [128, 8, TH])
    xtok_d = din("x_tok", [T, D])
    memT_d = din("memT", [128, 8, 256])
    w_in_d = din("w_in", [D, IN_W])
    w_kv_d = din("w_kv", [D, 2048])
    w_bs_d = din("w_br_ssd", [2048, D])
    w_bl_d = din("w_br_lru", [1536, D])
    w_bm_d = din("w_br_mem", [D, D])
    w_out_d = din("w_out", [D, D])
    wa_d = din("lru_wa_bd", [1536, 1536])
    wx_d = din("lru_wx_bd", [1536, 1536])
    pp_d = din("pp", [128, PP_N])
    pb_d = din("pb", [128, PB_N])
    cst_d = din("cst", [128, C_N])
    sel_d = din("sel2", [64, 32 * 128])
    di_d = din("dih", [128, 32 * 128])
    exs_d = din("exsel", [128, 20])
    out_d = nc.dram_tensor("out", [T, D], F32, kind="ExternalOutput").ap()

    def scratch(name, shape, dt):
        if DEBUG:
            return nc.dram_tensor(name, shape, dt, kind="ExternalOutput").ap()
        return nc.dram_tensor(name, shape, dt).ap()

    ylru_d = scratch("s_ylru", [1536, T], BF16)
    ymem_d = scratch("s_ymem", [D, T], BF16)
    yssd_d = scratch("s_yssd", [2048, T], BF16)
    la_d = scratch("s_la", [1536, T], BF16)
    u_d = scratch("s_u", [1536, T], BF16)
    xtm_d = scratch("s_xtm", [4, 128, NCH * 512], BF16)
    btm_d = scratch("s_btm", [4, 128, NCH * 128], BF16)
    bT_d = scratch("s_bT", [4, 128, T], BF16)
    cT_d = scratch("s_cT", [4, 128, T], BF16)
    cc_src = nc.dram_tensor("cc_src", [128, 2048], F32)
    cc_dst = nc.dram_tensor("cc_dst", [4 * 128, 2048], F32)
    cc_src_s = nc.dram_tensor("cc_src_s", [128, 64], F32)
    cc_dst_s = nc.dram_tensor("cc_dst_s", [4 * 128, 64], F32)
    dbg_d = scratch("s_dbg", [128, 8192], F32) if DEBUG else None
    ylru_dr = [Res() for _ in range(12)]
    ymem_dr = [Res() for _ in range(8)]
    yssd_dr = [Res() for _ in range(16)]
    la_dr = [Res() for _ in range(12)]
    u_dr = [Res() for _ in range(12)]
    grp_dr = [Res() for _ in range(4)]
    ccsrc_r = Res("ccsrc")
    ccdst_r = Res("ccdst")
    ccsrc_s_r = Res("ccsrc_s")
    ccdst_s_r = Res("ccdst_s")

    es = ExitStack()
    uid = [0]
    with es:
        def sb(name, shape, dt, st=None):
            uid[0] += 1
            return (st or es).enter_context(nc.sbuf_tensor(f"sb{uid[0]}_{name}", shape, dt))

        def ps(name, shape, dt):
            return es.enter_context(nc.psum_tensor("ps_" + name, shape, dt))

        class Phase:
            def __init__(self, name):
                self.name = name
                self.st = ExitStack()

            def __enter__(self):
                self.st.__enter__()
                return self

            def __exit__(self, *a):
                self.st.__exit__(*a)
                P.fence()
                return False

            def tile(self, name, shape, dt):
                return sb(name, shape, dt, self.st), Res(name)

        hT = sb("hT", [128, 8, TH], BF16)
        hT_r = Res("hT")
        pp = sb("pp", [128, PP_N], F32)
        pb = sb("pb", [128, PB_N], F32)
        cst = sb("cst", [128, C_N], F32)
        cstb = sb("cstb", [128, C_N], BF16)
        exs = sb("exs", [128, 20], F32)
        const_r = Res("const")
        ca_lru = sb("ca_lru", [128, 12], F32)
        small_r = Res("small")
        pay = sb("pay", [128, 64], F32)
        pay_r = Res("pay")
        lsum = sb("lsum", [128, 24], F32)
        lsum_r = Res("lsum")
        gat = sb("gat", [128, 4, 56], F32)
        gat_r = Res("gat")
        coef = sb("coef", [128, 4, 44], F32)
        coef_r = Res("coef")
        hin = sb("hin", [128, 12], F32)
        hin_r = Res("hin")
        w2s = sb("w2s", [128, 512], F32)
        decs = sb("decs", [128, 512], F32)
        ssm_r = Res("ssm")
        acsS = sb("acsS", [64, T], BF16)
        rowS = sb("rowS", [64, T], BF16)
        hl_r = Res("hilo")
        NW = 4
        wst = Ring([sb(f"wst{i}", [128, 1024], F32) for i in range(NW)])
        wbf = Ring([sb(f"wbf{i}", [128, 1024], BF16) for i in range(NW)])
        pair = [ps(f"pair{i}", [128, 1024], F32) for i in range(3)]
        pair_r = [[Res(f"pair{i}a"), Res(f"pair{i}b")] for i in range(3)]
        pbb = ps("pbb", [128, 1024], BF16)
        pbb_r = Res("pbb")
        pm = ps("pm", [128, 512], F32)
        pm_r = Res("pm")
        pair_i = [0]

        def next_pair():
            k = pair_i[0] % 3
            pair_i[0] += 1
            return pair[k], pair_r[k]

        sems = {e: es.enter_context(nc.semaphore(f"e_{e}")) for e in Prog.ENG}
        dsem = {e: [es.enter_context(nc.semaphore(f"d_{e}{k}")) for k in range(Prog.K)] for e in ("sp", "pool", "act")}
        cc_sem = es.enter_context(nc.semaphore("cc"))
        cc_sem_s = es.enter_context(nc.semaphore("ccs"))
        block = es.enter_context(nc.Block())

        def act(out, in_, func, reads, writes, bias=None, scale=None, accum=None):
            kw = {}
            if bias is not None:
                kw["bias"] = bias
            if scale is not None:
                kw["scale"] = scale
            if accum is not None:
                kw["accum_out"] = accum
            return P.op("act", lambda e: e.activation(out=out, in_=in_, func=func, **kw), reads, writes)

        def tt(eng, out, in0, in1, op, reads, writes):
            return P.op(eng, lambda e: e.tensor_tensor(out=out, in0=in0, in1=in1, op=op), reads, writes)

        def ts(eng, out, in0, s1, s2, op0, op1, reads, writes):
            if op1 is None:
                return P.op(eng, lambda e: e.tensor_scalar(out=out, in0=in0, scalar1=s1, scalar2=None, op0=op0), reads, writes)
            return P.op(eng, lambda e: e.tensor_scalar(out=out, in0=in0, scalar1=s1, scalar2=s2, op0=op0, op1=op1), reads, writes)

        def stt(out, in0, scalar, in1, op0, op1, reads, writes):
            return P.op("dve", lambda e: e.scalar_tensor_tensor(out=out, in0=in0, scalar=scalar, in1=in1, op0=op0, op1=op1), reads, writes)

        def copy(eng, out, in_, reads, writes):
            if eng == "act":
                return P.op("act", lambda e: e.copy(out=out, in_=in_), reads, writes)
            return P.op(eng, lambda e: e.tensor_copy(out=out, in_=in_), reads, writes)

        def mm(out, pairs, reads, writes, start=True, stop=True):
            def fn(e):
                n = len(pairs)
                ins = None
                for i, (l, r) in enumerate(pairs):
                    ins = e.matmul(out, lhsT=l, rhs=r, start=(start and i == 0), stop=(stop and i == n - 1))
                return ins
            return P.op("pe", fn, reads, writes)

        def transp(out, in_, ident, reads, writes):
            return P.op("pe", lambda e: e.transpose(out, in_, ident), reads, writes)

        cast_default = ["dve"]

        def wload(srcs, a, b, cast_eng=None, q="sp"):
            cast_eng = cast_eng or cast_default[0]
            s32, r32 = wst.next()
            s16, r16 = wbf.next()
            v32 = s32[:, 0:a * b].rearrange("p (a b) -> p a b", a=a)
            v16 = s16[:, 0:a * b].rearrange("p (a b) -> p a b", a=a)
            for (a0, a1, src) in srcs:
                P.dma(q, v32[:, a0:a1, :], src, writes=[r32])
            copy(cast_eng, v16, v32, [r32], [r16])
            return v16, r16

        w_in_v = w_in_d.rearrange("(kc p) n -> p kc n", p=128)

        def win_tile(c0, ncol):
            return wload([(0, 8, w_in_v[:, :, c0:c0 + ncol])], 8, ncol)

        def inproj_half(wv, wr, ncol, hf, dst=None):
            pr, prr = dst if dst is not None else next_pair()
            t0 = 3 + 1024 * hf
            for blk in range(2):
                pairs = [(wv[:, kc, 0:ncol], hT[:, kc, t0 + 512 * blk: t0 + 512 * blk + 512]) for kc in range(8)]
                mm(pr[0:ncol, 512 * blk: 512 * blk + 512], pairs, [wr, hT_r], [prr[blk]])
            return pr, prr

        def inproj_halo(wv, wr, ncol):
            pairs = [(wv[:, kc, 0:ncol], hT[:, kc, 0:3]) for kc in range(8)]
            mm(pm[0:ncol, 0:3], pairs, [wr, hT_r], [pm_r])

        def conv_A(wvr, wcol, out_f32, out_r, xr, xrr):
            wv, wr = wvr
            inproj_halo(wv, wr, 128)
            copy("act", xr[:, 0:3], pm[:, 0:3], [pm_r], [xrr])
            for hf in range(2):
                pr, prr = inproj_half(wv, wr, 128, hf)
                copy("act", xr[:, 3 + 1024 * hf: 3 + 1024 * hf + 1024], pr[:, :], [prr], [xrr])
                act(out_f32[:, 1024 * hf: 1024 * hf + 1024], pr[:, :], AF.Identity, [prr, const_r], [out_r[hf] if isinstance(out_r, list) else out_r],
                    bias=pp[:, wcol + 4: wcol + 5], scale=pp[:, wcol + 3: wcol + 4])

        def conv_B(wcol, out_f32, out_r, xr, xrr):
            w = lambda k: pp[:, wcol + k: wcol + k + 1]
            for hf in range(2):
                o = out_f32[:, 1024 * hf: 1024 * hf + 1024]
                b0 = 1024 * hf
                orr = out_r[hf] if isinstance(out_r, list) else out_r
                for k in (2, 1, 0):
                    stt(o, xr[:, b0 + k: b0 + k + 1024], w(k), o, ALU.mult, ALU.add, [xrr, orr, const_r], [orr])

        final_ops = []

        P.dma("sp", pp[:], pp_d, writes=[const_r])
        P.dma("sp", pb[:], pb_d, writes=[const_r])
        P.dma("sp", cst[:], cst_d, writes=[const_r])
        P.dma("sp", exs[:], exs_d, writes=[const_r])
        copy("pool", cstb[:], cst[:], [const_r], [const_r])
        P.op("dve", lambda e: e.memset(pay[:, :], 0.0), [], [pay_r])
        ident_b = cstb[:, C_ID:C_ID + 128]
        ones_b = cstb[:, C_ONE:C_ONE + 128]
        U_f = cst[:, C_U:C_U + 128]
        ones_f = cst[:, C_ONE:C_ONE + 128]
        neg_b = cstb[:, C_NEG:C_NEG + 128]
        blocks5 = [(0, 512), (512, 512), (1024, 512), (1536, 512), (2048, 3)]

        def ps_blk(bi):
            if bi < 2:
                return pair[0][:, 512 * bi: 512 * bi + 512], pair_r[0][bi]
            if bi < 4:
                return pair[1][:, 512 * (bi - 2): 512 * (bi - 2) + 512], pair_r[1][bi - 2]
            return pm[:, 0:3], pm_r

        def rms_featmajor(ph, src_d, ncols, gcol, dst, dst_r, blocks):
            xs = [ph.tile(f"x{kc}", [128, ncols], F32) for kc in range(8)]
            sqs = [ph.tile(f"sq{i}", [128, ncols], BF16) for i in range(2)]
            rstd, rstd_r = ph.tile("rstd", [128, ncols], F32)
            for kc in range(8):
                P.dma("sp", xs[kc][0][:, :], src_d[:, kc, :], writes=[xs[kc][1]])
            for kc in range(8):
                sq, sqr = sqs[kc % 2]
                act(sq[:, :], xs[kc][0][:, :], AF.Square, [xs[kc][1]], [sqr])
                for (o, orr, c0, n) in blocks:
                    P.op("pe", (lambda o=o, sq=sq, c0=c0, n=n, kc=kc: (lambda e: e.matmul(o[:, 0:n], lhsT=ones_b, rhs=sq[:, c0:c0 + n], start=(kc == 0), stop=(kc == 7))))(),
                         [sqr, const_r], [orr])
            for (o, orr, c0, n) in blocks:
                ts("dve", rstd[:, c0:c0 + n], o[:, 0:n], 1.0 / D, EPS, ALU.mult, ALU.add, [orr], [rstd_r])
            act(rstd[:, :], rstd[:, :], AF.Sqrt, [rstd_r], [rstd_r])
            P.op("dve", lambda e: e.reciprocal(out=rstd[:, :], in_=rstd[:, :]), [rstd_r], [rstd_r])
            for kc in range(8):
                stt(dst[:, kc, :], xs[kc][0][:, :], pp[:, gcol + kc: gcol + kc + 1], rstd[:, :],
                    ALU.mult, ALU.mult, [xs[kc][1], rstd_r, const_r], [dst_r])

        with Phase("p0") as ph:
            blks = []
            for bi, (c0, n) in enumerate(blocks5):
                o, orr = ps_blk(bi)
                blks.append((o, orr, c0, n))
            rms_featmajor(ph, xT_d, TH, PP_NORMG, hT, hT_r, blks)
            act(ca_lru[:], pp[:, PP_LAM:PP_LAM + 12], AF.Exp, [const_r], [small_r], scale=-1.0)
            act(ca_lru[:], ca_lru[:], AF.Ln, [small_r], [small_r], bias=1.0)
            ts("dve", ca_lru[:], ca_lru[:], -8.0, None, ALU.mult, None, [small_r], [small_r])

        def lru_tiles_for(jo):
            b0 = (128 * jo) // 96
            b1 = (128 * jo + 127) // 96
            return (96 * b0) // 128, (96 * (b1 + 1) - 1) // 128

        wa_v = wa_d.rearrange("(i p) n -> p i n", p=128)
        wx_v = wx_d.rearrange("(i p) n -> p i n", p=128)

        if "lru1" not in STOP:
          cast_default[0] = "act"
          with Phase("lru1") as ph:
            xl_ring = [(ph.tile(f"xl{i}", [128, T], F32)[0], [Res(), Res()]) for i in range(3)]
            xlb_ring = [(ph.tile(f"xlb{i}", [128, T], BF16)[0], [Res(), Res()]) for i in range(3)]
            xraw = [ph.tile(f"xraw{i}", [128, TH], F32) for i in range(2)]
            tR = [ph.tile(f"tR{i}", [128, 1024], F32) for i in range(3)]
            tI = [ph.tile(f"tI{i}", [128, 1024], F32) for i in range(3)]
            tA = [ph.tile(f"tA{i}", [128, 1024], F32) for i in range(3)]
            tE = [ph.tile(f"tE{i}", [128, 1024], F32) for i in range(3)]
            labs = [ph.tile(f"lab{i}", [128, T], BF16) for i in range(2)]
            ubs = [ph.tile(f"ub{i}", [128, T], BF16) for i in range(2)]
            hctr = [0]
            cah, cah_r = ph.tile("cah", [128, 12], F32)
            ts("dve", cah[:, :], ca_lru[:, :], 0.5, None, ALU.mult, None, [small_r], [cah_r])

            def lru_gates(jo):
                i0, i1 = lru_tiles_for(jo)
                ni = i1 - i0 + 1
                wav, war = wload([(0, ni, wa_v[:, i0:i1 + 1, jo * 128:(jo + 1) * 128])], ni, 128)
                wxv, wxr = wload([(0, ni, wx_v[:, i0:i1 + 1, jo * 128:(jo + 1) * 128])], ni, 128)
                xl, xlr = xl_ring[jo % 3]
                lab, lab_r = labs[jo % 2]
                ub, ub_r = ubs[jo % 2]
                prev_h = None
                st = []
                for hf in range(2):
                    tk = slice(1024 * hf, 1024 * hf + 1024)
                    k = hctr[0]
                    hctr[0] += 1
                    (R_, R_r), (I_, I_r), (A_, A_r), (E_, E_r) = tR[k % 3], tI[k % 3], tA[k % 3], tE[k % 3]
                    pr_r, pr_rr = next_pair()
                    pr_i, pr_ir = next_pair()
                    for blk in range(2):
                        cs = slice(1024 * hf + 512 * blk, 1024 * hf + 512 * blk + 512)
                        os_ = slice(512 * blk, 512 * blk + 512)
                        mm(pr_r[:, os_], [(wav[:, i - i0, :], xlb_ring[i % 3][0][:, cs]) for i in range(i0, i1 + 1)],
                           [war] + [xlb_ring[i % 3][1][hf] for i in range(i0, i1 + 1)], [pr_rr[blk]])
                        mm(pr_i[:, os_], [(wxv[:, i - i0, :], xlb_ring[i % 3][0][:, cs]) for i in range(i0, i1 + 1)],
                           [wxr] + [xlb_ring[i % 3][1][hf] for i in range(i0, i1 + 1)], [pr_ir[blk]])
                    act(R_[:, :], pr_r[:, :], AF.Tanh, [pr_rr, const_r], [R_r], bias=bh[:, jo:jo + 1], scale=0.5)
                    act(I_[:, :], pr_i[:, :], AF.Tanh, [pr_ir, const_r], [I_r], bias=bh[:, 12 + jo:13 + jo], scale=0.5)
                    ts("dve", lab[:, tk], R_[:, :], cah[:, jo:jo + 1], cah[:, jo:jo + 1], ALU.mult, ALU.add, [R_r, cah_r], [lab_r])
                    act(E_[:, :], lab[:, tk], AF.Exp, [lab_r], [E_r], scale=2.0)
                    act(A_[:, :], lab[:, tk], AF.Exp, [lab_r], [A_r])
                    stt(I_[:, :], I_[:, :], 1.0, xl[:, tk], ALU.add, ALU.mult, [I_r, xlr[hf]], [I_r])
                    ts("pool", E_[:, :], E_[:, :], -1.0, 1.0, ALU.mult, ALU.add, [E_r], [E_r])
                    st.append((tk, R_, R_r, I_, I_r, A_, A_r, E_, E_r))
                for hf, (tk, R_, R_r, I_, I_r, A_, A_r, E_, E_r) in enumerate(st):
                    act(E_[:, :], E_[:, :], AF.Sqrt, [E_r], [E_r])
                    stt(ub[:, tk], E_[:, :], 0.5, I_[:, :], ALU.mult, ALU.mult, [E_r, I_r], [ub_r])
                    init = 0.0 if hf == 0 else prev_h[:, 1023:1024]
                    rd = [A_r, ub_r, R_r] + ([st[0][2]] if hf == 1 else [])
                    P.op("dve", (lambda tk=tk, init=init, R_=R_, A_=A_, ub=ub: (lambda e: e.tensor_tensor_scan(out=R_[:, :], data0=A_[:, :], data1=ub[:, tk], initial=init, op0=ALU.mult, op1=ALU.add)))(),
                         rd, [R_r])
                    prev_h = R_
                    P.op("dve", (lambda tk=tk, hf=hf, lab=lab: (lambda e: e.reduce_sum(out=lsum[:, 2 * jo + hf: 2 * jo + hf + 1], in_=lab[:, tk], axis=AX.X)))(),
                         [lab_r], [lsum_r])
                copy("dve", pay[:, 44 + jo: 45 + jo], prev_h[:, 1023:1024], [st[1][2]], [pay_r])
                P.dma("pool", la_d[jo * 128:(jo + 1) * 128, :], lab[:, 0:T], reads=[lab_r], writes=[la_dr[jo]])
                P.dma("pool", u_d[jo * 128:(jo + 1) * 128, :], ub[:, 0:T], reads=[ub_r], writes=[u_dr[jo]])

            bh, bh_r = ph.tile("bh", [128, 24], F32)
            ts("dve", bh[:, :], pp[:, PP_LBA:PP_LBA + 24], 0.5, None, ALU.mult, None, [const_r], [const_r])
            lw = {}

            def lruW(j):
                if j < 12:
                    lw[j] = win_tile(COL_LX + 128 * j, 128)

            def lruA(j):
                conv_A(lw.pop(j), PP_LCONV + 5 * j, xl_ring[j % 3][0], xl_ring[j % 3][1], xraw[j % 2][0], xraw[j % 2][1])

            lruW(0)
            lruW(1)
            lruA(0)
            for j in range(12):
                xr, xrr = xraw[j % 2]
                xl, xlr = xl_ring[j % 3]
                xlb, xlbr = xlb_ring[j % 3]
                lruW(j + 2)
                if j + 1 < 12:
                    lruA(j + 1)
                conv_B(PP_LCONV + 5 * j, xl, xlr, xr, xrr)
                copy("act", xlb[:, 0:1024], xl[:, 0:1024], [xlr[0]], [xlbr[0]])
                copy("pool", xlb[:, 1024:T], xl[:, 1024:T], [xlr[1]], [xlbr[1]])
                if j >= 1:
                    lru_gates(j - 1)
            lru_gates(11)
            lv = lsum[:].rearrange("p (j two) -> p j two", two=2)
            tt("dve", pay[:, 32:44], lv[:, :, 0], lv[:, :, 1], ALU.add, [lsum_r], [pay_r])
          cast_default[0] = "dve"

        if "ssd1" not in STOP:
          cast_default[0] = "act"
          with Phase("ssd1") as ph:
            xraw = [ph.tile(f"xraw{i}", [128, TH], F32) for i in range(2)]
            acc = [ph.tile(f"acc{i}", [128, T], F32) for i in range(2)]
            fm = [ph.tile(f"fm{i}", [128, T], BF16) for i in range(4)]
            xtm = [ph.tile(f"xtm{i}", [128, NCH, 512], BF16) for i in range(2)]
            btm = [ph.tile(f"btm{i}", [128, NCH, 128], BF16) for i in range(2)]
            xw1 = [ph.tile(f"xw{i}", [128, 512], BF16) for i in range(3)]
            sloc = [ph.tile(f"sloc{i}", [128, 512], F32) for i in range(2)]
            dt_tm, dtm_r = ph.tile("dt_tm", [128, 512], F32)
            dtA_tm, dta_r = ph.tile("dtA_tm", [128, 512], F32)
            acs_tm, acs_r = ph.tile("acs_tm", [128, 512], F32)
            tot_bc, tot_r = ph.tile("tot_bc", [128, 512], F32)
            pre, pre_r = ph.tile("pre", [128, 512], F32)
            w1, w1_r = ph.tile("w1", [128, 512], F32)
            anegb, anegb_r = ph.tile("anegb", [128, 32], F32)
            anegp, anegp_r = ph.tile("anegp", [64, 1], F32)
            dtbp, dtbp_r = ph.tile("dtbp", [64, 1], F32)
            logd, logd_r = ph.tile("logd", [128, 32], F32)
            v3 = lambda t: t[:, :].rearrange("p (c h) -> p c h", c=NCH)

            wdv, wdr = wload([(0, 8, w_in_v[:, :, COL_DT:COL_DT + 32])], 8, 32)
            prd, prd_r = next_pair()
            for c in range(NCH):
                mm(prd[:, 32 * c:32 * c + 32], [(hT[:, kc, 3 + 128 * c: 3 + 128 * c + 128], wdv[:, kc, :]) for kc in range(8)], [wdr, hT_r], [prd_r])
            tt("dve", v3(dt_tm), prd[:, 0:512].rearrange("p (c h) -> p c h", c=NCH),
               pb[:, PB_DTB:PB_DTB + 32].unsqueeze(1).to_broadcast([128, NCH, 32]), ALU.add, [prd_r, const_r], [dtm_r])
            act(dt_tm[:, :], dt_tm[:, :], AF.Exp, [dtm_r], [dtm_r])
            act(dt_tm[:, :], dt_tm[:, :], AF.Ln, [dtm_r], [dtm_r], bias=1.0)
            act(anegb[:, :], pb[:, PB_ALOG:PB_ALOG + 32], AF.Exp, [const_r], [anegb_r])
            ts("dve", anegb[:, :], anegb[:, :], -1.0, None, ALU.mult, None, [anegb_r], [anegb_r])
            tt("dve", v3(dtA_tm), v3(dt_tm), anegb[:, :].unsqueeze(1).to_broadcast([128, NCH, 32]), ALU.mult, [dtm_r, anegb_r], [dta_r])
            pr2, pr2_r = next_pair()
            mm(pr2[:, 0:512], [(U_f, dtA_tm[:, :])], [dta_r, const_r], [pr2_r])
            mm(pr2[:, 512:1024], [(ones_f, dtA_tm[:, :])], [dta_r, const_r], [pr2_r])
            copy("act", acs_tm[:, :], pr2[:, 0:512], [pr2_r], [acs_r])
            copy("act", tot_bc[:, :], pr2[:, 512:1024], [pr2_r], [tot_r])
            tt("dve", w2s[:, :], tot_bc[:, :], acs_tm[:, :], ALU.subtract, [tot_r, acs_r], [ssm_r])
            act(w2s[:, :], w2s[:, :], AF.Exp, [ssm_r], [ssm_r])
            tt("dve", w2s[:, :], w2s[:, :], dt_tm[:, :], ALU.mult, [ssm_r, dtm_r], [ssm_r])
            act(decs[:, :], tot_bc[:, :], AF.Exp, [tot_r], [ssm_r])
            P.op("dve", lambda e: e.memset(pre[:, 0:32], 0.0), [], [pre_r])
            for c in range(1, NCH):
                tt("dve", pre[:, 32 * c:32 * c + 32], pre[:, 32 * (c - 1):32 * c], tot_bc[:, 32 * (c - 1):32 * c], ALU.add, [pre_r, tot_r], [pre_r])
            tt("dve", logd[:, :], pre[:, 480:512], tot_bc[:, 480:512], ALU.add, [pre_r, tot_r], [logd_r])
            copy("dve", pay[:, 0:32], logd[:, :], [logd_r], [pay_r])
            tt("dve", v3(w1), v3(pre), logd[:, :].unsqueeze(1).to_broadcast([128, NCH, 32]), ALU.subtract, [logd_r, pre_r], [w1_r])
            tt("dve", w1[:, :], w1[:, :], acs_tm[:, :], ALU.add, [w1_r, acs_r], [w1_r])
            act(w1[:, :], w1[:, :], AF.Exp, [w1_r], [w1_r], scale=-1.0)
            tt("dve", w1[:, :], w1[:, :], dt_tm[:, :], ALU.mult, [w1_r, dtm_r], [w1_r])

            f_dt, f_dt_r = xraw[0]
            f_ln, f_ln_r = xraw[1]
            f_ac, f_ac_r = acc[0]
            f_t, f_t_r = acc[1]
            for hf in range(2):
                pr, prr = next_pair()
                t0 = 3 + 1024 * hf
                for blk in range(2):
                    pairs = []
                    for kc in range(8):
                        pairs.append((wdv[:, kc, :], hT[:, kc, t0 + 512 * blk: t0 + 512 * blk + 512]))
                    mm(pr[0:32, 512 * blk:512 * blk + 512], pairs, [wdr, hT_r], [prr])
                    mm(pr[32:64, 512 * blk:512 * blk + 512], pairs, [wdr, hT_r], [prr])
                act(f_dt[0:64, 1024 * hf:1024 * hf + 1024], pr[0:64, :], AF.Exp, [prr, const_r], [f_dt_r], bias=pp[0:64, PP_DTB:PP_DTB + 1])
            act(f_dt[0:64, 0:T], f_dt[0:64, 0:T], AF.Ln, [f_dt_r], [f_dt_r], bias=1.0)
            act(f_ln[0:64, 0:T], f_dt[0:64, 0:T], AF.Ln, [f_dt_r], [f_ln_r])
            dtA2, dtA2_r = ph.tile("dtA2", [128, NCH, 64], F32)
            copy("pool", dtA2[:, :, 0:32], v3(dtA_tm), [dta_r], [dtA2_r])
            copy("pool", dtA2[:, :, 32:64], v3(dtA_tm), [dta_r], [dtA2_r])
            for q4 in range(4):
                pr, prr = next_pair()
                for cc in range(4):
                    c = 4 * q4 + cc
                    mm(pr[0:64, 128 * cc:128 * cc + 128], [(dtA2[:, c, :], U_f)], [dtA2_r, const_r], [prr])
                copy("act", f_ac[0:64, 512 * q4:512 * q4 + 512], pr[0:64, 0:512], [prr], [f_ac_r])
            tt("dve", f_ln[0:64, 0:T], f_ln[0:64, 0:T], f_ac[0:64, 0:T], ALU.subtract, [f_ln_r, f_ac_r], [f_ln_r])

            def hilo(src, src_r, dstS):
                copy("dve", dstS[0:32, :], src[0:32, 0:T], [src_r], [hl_r])
                hb, hb_r = fm[0]
                copy("dve", hb[32:64, 0:T], src[32:64, 0:T], [src_r], [hb_r])
                tt("dve", f_t[32:64, 0:T], src[32:64, 0:T], hb[32:64, 0:T], ALU.subtract, [src_r, hb_r], [f_t_r])
                copy("dve", dstS[32:64, :], f_t[32:64, 0:T], [f_t_r], [hl_r])
            hilo(f_ac, f_ac_r, acsS)
            hilo(f_ln, f_ln_r, rowS)

            tiles = []
            for g in range(4):
                tiles += [(g, COL_X + 512 * g + 128 * i, 4 * g + i, "x", i) for i in range(4)]
                tiles += [(g, COL_B + 128 * g, 16 + g, "b", 0), (g, COL_C + 128 * g, 20 + g, "c", 0)]
            pbb_h = [Res("pbbA"), Res("pbbB")]

            sw = {}

            def ssdW(k):
                if k < len(tiles):
                    sw[k] = win_tile(tiles[k][1], 128)

            def ssdA(k):
                g, c0, tix, kind, i = tiles[k]
                conv_A(sw.pop(k), PP_SCONV + 5 * tix, acc[k % 2][0], acc[k % 2][1], xraw[k % 2][0], xraw[k % 2][1])

            ssdW(0)
            ssdW(1)
            ssdA(0)
            tq = [0]
            for k, (g, c0, tix, kind, i) in enumerate(tiles):
                xt, xt_r = xtm[g % 2]
                bt, bt_r = btm[g % 2]
                xr, xrr = xraw[k % 2]
                ac, ac_r = acc[k % 2]
                f, f_r = fm[k % 4]
                ssdW(k + 2)
                if k + 1 < len(tiles):
                    ssdA(k + 1)
                conv_B(PP_SCONV + 5 * tix, ac, ac_r, xr, xrr)
                act(f[:, 0:T], ac[:, 0:T], AF.Silu, [ac_r], [f_r])
                if kind in ("x", "b"):
                    for half in range(2):
                        for cc in range(8):
                            c = 8 * half + cc
                            transp(pbb[:, 128 * cc:128 * cc + 128], f[:, 128 * c:128 * c + 128], ident_b, [f_r, const_r], [pbb_r])
                        src = pbb[:, 0:1024].rearrange("p (c n) -> p c n", c=8)
                        if kind == "x":
                            copy("dve", xt[:, 8 * half:8 * half + 8, 128 * i:128 * i + 128], src, [pbb_r], [xt_r])
                        else:
                            copy("dve", bt[:, 8 * half:8 * half + 8, :], src, [pbb_r], [bt_r])
                if kind == "b":
                    P.dma("pool", bT_d[g], f[:, 0:T], reads=[f_r], writes=[grp_dr[g]])
                if kind == "c":
                    P.dma("pool", cT_d[g], f[:, 0:T], reads=[f_r], writes=[grp_dr[g]])
                    prs, prs_r = next_pair()
                    for c in range(NCH):
                        xw, xw_r = xw1[c % 3]
                        tt("dve" if c % 2 == 0 else "pool", xw[:, :].rearrange("p (h q) -> p h q", h=8), xt[:, c, :].rearrange("p (h q) -> p h q", h=8),
                           w1[:, 32 * c + 8 * g:32 * c + 8 * g + 8].unsqueeze(2).to_broadcast([128, 8, 64]), ALU.mult, [xt_r, w1_r], [xw_r])
                        mm(prs[:, 0:512], [(bt[:, c, :], xw[:, :])], [bt_r, xw_r], [prs_r[0]], start=(c == 0), stop=(c == NCH - 1))
                    sl, sl_r = sloc[g % 2]
                    copy("act", sl[:, :], prs[:, 0:512], [prs_r[0]], [sl_r])
                    P.dma("pool", cc_src.ap()[:, 512 * g:512 * g + 512], sl[:, :], reads=[sl_r], writes=[ccsrc_r])
                    P.dma("pool", xtm_d[g], xt[:, :, :].rearrange("p c n -> p (c n)"), reads=[xt_r], writes=[grp_dr[g]])
                    P.dma("pool", btm_d[g], bt[:, :, :].rearrange("p c n -> p (c n)"), reads=[bt_r], writes=[grp_dr[g]])

        cast_default[0] = "dve"
        if "xchg" not in STOP:
            P.dma("pool", cc_src_s.ap()[:, :], pay[:, :], reads=[pay_r], writes=[ccsrc_s_r])
            P.custom("pool", lambda e: e.collective_compute("AllGather", ALU.bypass, replica_groups=[[0, 1, 2, 3], [4, 5, 6, 7]],
                                                            ins=[cc_src_s.ap().opt()], outs=[cc_dst_s.ap().opt()]),
                     cc_sem_s, reads=[ccsrc_s_r], writes=[ccdst_s_r])
            cc_op = P.custom("pool", lambda e: e.collective_compute("AllGather", ALU.bypass, replica_groups=[[0, 1, 2, 3], [4, 5, 6, 7]],
                                                                    ins=[cc_src.ap().opt()], outs=[cc_dst.ap().opt()]),
                             cc_sem, reads=[ccsrc_r], writes=[ccdst_r])

        w_kv_v = w_kv_d.rearrange("(kc p) n -> p kc n", p=128)
        if "attn" not in STOP:
          with Phase("attn") as ph:
            mnT, mnT_r = ph.tile("mnT", [128, 8, 256], BF16)
            kT, kT_r = ph.tile("kT", [128, 8, 256], BF16)
            Vt, V_r = ph.tile("V", [128, 2, 1024], BF16)
            qT = [ph.tile(f"qT{i}", [128, 2, T], BF16) for i in range(2)]
            PT = [ph.tile(f"PT{i}", [128, 2, 512], BF16) for i in range(2)]
            rinv = [ph.tile(f"rinv{i}", [128, 512], F32) for i in range(2)]
            yo = [ph.tile(f"yo{i}", [128, T], BF16) for i in range(2)]
            rms_featmajor(ph, memT_d, 256, PP_MEMG, mnT, mnT_r, [(pair[0][:, 0:256], pair_r[0], 0, 256)])
            for dtile in range(8):
                wv, wr = wload([(0, 8, w_kv_v[:, :, 128 * dtile:128 * dtile + 128])], 8, 128)
                if dtile % 4 == 0:
                    pr, prr = next_pair()
                o = pr[:, 256 * (dtile % 4):256 * (dtile % 4) + 256]
                mm(o, [(wv[:, kc, :], mnT[:, kc, :]) for kc in range(8)], [wr, mnT_r], [prr])
                if dtile % 4 == 3:
                    copy("act", kT[:, dtile - 3:dtile + 1, :], pr[:, :].rearrange("p (a m) -> p a m", a=4), [prr], [kT_r])
            for dvt in range(8):
                wv, wr = wload([(0, 8, w_kv_v[:, :, 1024 + 128 * dvt:1024 + 128 * dvt + 128])], 8, 128)
                if dvt % 4 == 0:
                    pr, prr = next_pair()
                for mt in range(2):
                    o = pr[:, 512 * mt + 128 * (dvt % 4):512 * mt + 128 * (dvt % 4) + 128]
                    mm(o, [(mnT[:, kc, 128 * mt:128 * mt + 128], wv[:, kc, :]) for kc in range(8)], [wr, mnT_r], [prr])
                if dvt % 4 == 3:
                    copy("act", Vt[:, :, 512 * (dvt // 4):512 * (dvt // 4) + 512], pr[:, :].rearrange("p (mt n) -> p mt n", mt=2), [prr], [V_r])
            for hd in range(4):
                q, q_r = qT[hd % 2]
                for dc in range(2):
                    wv, wr = win_tile(COL_Q + 256 * hd + 128 * dc, 128)
                    for hf in range(2):
                        pr, prr = inproj_half(wv, wr, 128, hf)
                        copy("act", q[:, dc, 1024 * hf:1024 * hf + 1024], pr[:, :], [prr], [q_r])
                y0, y0_r = yo[0]
                y1, y1_r = yo[1]
                for tb in range(4):
                    tk = slice(512 * tb, 512 * tb + 512)
                    pt, pt_r = PT[tb % 2]
                    prs, prs_r = next_pair()
                    for mt in range(2):
                        mm(prs[:, 512 * mt:512 * mt + 512], [(kT[:, 2 * hd + dc, 128 * mt:128 * mt + 128], q[:, dc, tk]) for dc in range(2)], [kT_r, q_r], [prs_r])
                    act(pt[:, :, :], prs[:, :].rearrange("p (mt n) -> p mt n", mt=2), AF.Exp, [prs_r], [pt_r], scale=1.0 / 16.0)
                    pro, pro_r = next_pair()
                    P.op("pe", (lambda pt=pt: (lambda e: e.matmul(pm[:, 0:512], lhsT=ones_b, rhs=pt[:, 0, :], start=True, stop=False)))(), [pt_r, const_r], [pm_r])
                    P.op("pe", (lambda pt=pt: (lambda e: e.matmul(pm[:, 0:512], lhsT=ones_b, rhs=pt[:, 1, :], start=False, stop=True)))(), [pt_r, const_r], [pm_r])
                    for dvt in range(2):
                        mm(pro[:, 512 * dvt:512 * dvt + 512], [(Vt[:, mt, 256 * hd + 128 * dvt:256 * hd + 128 * dvt + 128], pt[:, mt, :]) for mt in range(2)], [V_r, pt_r], [pro_r])
                    ri, ri_r = rinv[tb % 2]
                    P.op("dve", (lambda ri=ri: (lambda e: e.reciprocal(out=ri[:, :], in_=pm[:, 0:512])))(), [pm_r], [ri_r])
                    tt("dve", y0[:, tk], pro[:, 0:512], ri[:, :], ALU.mult, [pro_r, ri_r], [y0_r])
                    tt("dve", y1[:, tk], pro[:, 512:1024], ri[:, :], ALU.mult, [pro_r, ri_r], [y1_r])
                for dvt, (yy, yy_r) in enumerate(((y0, y0_r), (y1, y1_r))):
                    P.dma("pool", ymem_d[256 * hd + 128 * dvt:256 * hd + 128 * dvt + 128, :], yy[:, 0:T], reads=[yy_r], writes=[ymem_dr[2 * hd + dvt]])

        if "xchg" not in STOP:
            cc_v = cc_dst.ap().rearrange("(r p) f -> p r f", p=128)
            cc_vs = cc_dst_s.ap().rearrange("(r p) f -> p r f", p=128)
            P.dma("sp", gat[:, :, :], cc_vs[:, :, 0:56], reads=[ccdst_s_r], writes=[gat_r])
            for j in range(4):
                for m in range(4):
                    sc = exs[:, 4 + 4 * j + m:5 + 4 * j + m]
                    if m == 0:
                        ts("dve", coef[:, j, :], gat[:, m, 0:44], sc, None, ALU.mult, None, [gat_r, const_r], [coef_r])
                    else:
                        stt(coef[:, j, :], gat[:, m, 0:44], sc, coef[:, j, :], ALU.mult, ALU.add, [gat_r, coef_r, const_r], [coef_r])
            act(coef[:, :, :], coef[:, :, :], AF.Exp, [coef_r], [coef_r])
            for j in range(4):
                ts("dve", coef[:, j, :], coef[:, j, :], exs[:, j:j + 1], None, ALU.mult, None, [coef_r, const_r], [coef_r])
            for j in range(4):
                if j == 0:
                    tt("dve", hin[:, :], coef[:, j, 32:44], gat[:, j, 44:56], ALU.mult, [coef_r, gat_r], [hin_r])
                else:
                    tt("dve", lsum[:, 0:12], coef[:, j, 32:44], gat[:, j, 44:56], ALU.mult, [coef_r, gat_r], [lsum_r])
                    tt("dve", hin[:, :], hin[:, :], lsum[:, 0:12], ALU.add, [hin_r, lsum_r], [hin_r])

        if "lru2" not in STOP:
          with Phase("lru2") as ph:
            laL = [ph.tile(f"laL{i}", [128, T], BF16) for i in range(2)]
            uL = [ph.tile(f"uL{i}", [128, T], BF16) for i in range(2)]
            sg = [ph.tile(f"sg{i}", [128, T], F32) for i in range(2)]
            aa = [ph.tile(f"aa{i}", [128, T], F32) for i in range(2)]
            hh = [ph.tile(f"hh{i}", [128, T], F32) for i in range(2)]
            yb = [ph.tile(f"yb{i}", [128, T], BF16) for i in range(2)]
            l2w = {0: win_tile(COL_LG, 128)}
            for j in range(12):
                la_t, la_r = laL[j % 2]
                u_t, u_r = uL[j % 2]
                P.dma("sp", la_t[:, :], la_d[128 * j:128 * j + 128, :], reads=[la_dr[j]], writes=[la_r])
                P.dma("sp", u_t[:, :], u_d[128 * j:128 * j + 128, :], reads=[u_dr[j]], writes=[u_r])
                if j + 1 < 12:
                    l2w[j + 1] = win_tile(COL_LG + 128 * (j + 1), 128)
                wv, wr = l2w.pop(j)
                s_t, s_r = sg[j % 2]
                a_t, a_r = aa[j % 2]
                h_t, h_r = hh[j % 2]
                y_t, y_r = yb[j % 2]
                for hf in range(2):
                    tk = slice(1024 * hf, 1024 * hf + 1024)
                    pr, prr = inproj_half(wv, wr, 128, hf)
                    act(s_t[:, tk], pr[:, :], AF.Silu, [prr], [s_r])
                for hf in range(2):
                    tk = slice(1024 * hf, 1024 * hf + 1024)
                    act(a_t[:, tk], la_t[:, tk], AF.Exp, [la_r], [a_r])
                for hf in range(2):
                    tk = slice(1024 * hf, 1024 * hf + 1024)
                    init = hin[:, j:j + 1] if hf == 0 else h_t[:, 1023:1024]
                    P.op("dve", (lambda tk=tk, init=init, h_t=h_t, a_t=a_t, u_t=u_t: (lambda e: e.tensor_tensor_scan(out=h_t[:, tk], data0=a_t[:, tk], data1=u_t[:, tk], initial=init, op0=ALU.mult, op1=ALU.add)))(),
                         [a_r, u_r, h_r, hin_r], [h_r])
                    tt("pool", y_t[:, tk], h_t[:, tk], s_t[:, tk], ALU.mult, [h_r, s_r], [y_r])
                P.dma("pool", ylru_d[128 * j:128 * j + 128, :], y_t[:, :], reads=[y_r], writes=[ylru_dr[j]])

        def load_cast(dst, src, nparts, ncols, res):
            for c0 in range(0, ncols, 1024):
                n = min(1024, ncols - c0)
                s32, r32 = wst.next()
                P.dma("sp", s32[0:nparts, 0:n], src[:, c0:c0 + n], writes=[r32])
                copy("pool", dst[0:nparts, c0:c0 + n], s32[0:nparts, 0:n], [r32], [res])

        if "ssd2" not in STOP:
          with Phase("ssd2") as ph:
            selb, selb_r = ph.tile("selb", [64, 32 * 128], BF16)
            dib, dib_r = ph.tile("dib", [128, 32 * 128], BF16)
            xt, xt_r = ph.tile("xt", [128, NCH, 512], BF16)
            bt, bt_r = ph.tile("bt", [128, NCH, 128], BF16)
            bTt, bTt_r = ph.tile("bTt", [128, T], BF16)
            cTt, cTt_r = ph.tile("cTt", [128, T], BF16)
            S, S_r = ph.tile("S", [128, 512], F32)
            Sd, Sd_r = ph.tile("Sd", [128, 512], F32)
            Sbs = [ph.tile(f"Sb{i}", [128, 512], BF16) for i in range(2)]
            Sj = [ph.tile(f"Sj{i}", [128, 512], F32) for i in range(2)]
            Stmp, Stmp_r = ph.tile("Stmp", [128, 512], F32)
            CBm = [ph.tile(f"CBm{i}", [128, 128], BF16) for i in range(3)]
            Esb = [ph.tile(f"Esb{i}", [128, 512], BF16) for i in range(4)]
            Mp = [ph.tile(f"Mp{i}", [128, 4, 128], BF16) for i in range(4)]
            EAsb = [ph.tile(f"EAsb{i}", [128, 512], BF16) for i in range(4)]
            CE = [ph.tile(f"CE{i}", [128, 4, 128], BF16) for i in range(4)]
            xw2 = [ph.tile(f"xw2{i}", [128, 512], BF16) for i in range(3)]
            Rt = [ph.tile(f"Rt{i}", [64, 4, 128], BF16) for i in range(6)]
            neg4, neg4_r = ph.tile("neg4", [128, 4, 128], BF16)
            for k4 in range(4):
                copy("pool", neg4[:, k4, :], neg_b, [const_r], [neg4_r])
            sz4, sz4_r = ph.tile("sz4", [128, 4, T], BF16)
            yz4, yz4_r = ph.tile("yz4", [128, 4, T], BF16)
            rstdg, rstdg_r = ph.tile("rstdg", [128, T], F32)
            yo = [ph.tile(f"yo{i}", [128, T], BF16) for i in range(1)]
            sqz = yo
            load_cast(selb, sel_d, 64, 32 * 128, selb_r)
            load_cast(dib, di_d, 128, 32 * 128, dib_r)
            cc_rows = cc_dst.ap()
            ybank, yb_r = pair[2][:, 0:512], pair_r[2][0]
            stp, stp_r = pair[2][:, 512:1024], pair_r[2][1]
            v8 = lambda t: t.rearrange("p (h q) -> p h q", h=8)
            def zproj(g):
                for i in range(4):
                    wv, wr = win_tile(COL_Z + 512 * g + 128 * i, 128)
                    for hf in range(2):
                        pr, prr = inproj_half(wv, wr, 128, hf)
                        act(sz4[:, i, 1024 * hf:1024 * hf + 1024], pr[:, :], AF.Silu, [prr], [sz4_r])

            zproj(0)
            for g in range(4):
                P.dma("sp", xt[:, :, :].rearrange("p c n -> p (c n)"), xtm_d[g], reads=[grp_dr[g]], writes=[xt_r])
                P.dma("sp", bt[:, :, :].rearrange("p c n -> p (c n)"), btm_d[g], reads=[grp_dr[g]], writes=[bt_r])
                P.dma("sp", bTt[:, :], bT_d[g], reads=[grp_dr[g]], writes=[bTt_r])
                P.dma("sp", cTt[:, :], cT_d[g], reads=[grp_dr[g]], writes=[cTt_r])
                for j in range(4):
                    sj, sj_r = Sj[j % 2]
                    P.dma("sp", sj[:, :], cc_rows[128 * j:128 * j + 128, 512 * g:512 * g + 512], reads=[ccdst_r], writes=[sj_r])
                    cb = coef[:, j, 8 * g:8 * g + 8].unsqueeze(2).to_broadcast([128, 8, 64])
                    if j == 0:
                        tt("dve", v8(S[:, :]), v8(sj[:, :]), cb, ALU.mult, [sj_r, coef_r], [S_r])
                    else:
                        tt("dve", v8(Stmp[:, :]), v8(sj[:, :]), cb, ALU.mult, [sj_r, coef_r], [Stmp_r])
                        tt("dve", S[:, :], S[:, :], Stmp[:, :], ALU.add, [S_r, Stmp_r], [S_r])
                sb0, sb0_r = Sbs[0]
                copy("dve", sb0[:, :], S[:, :], [S_r], [sb0_r])

                def EXa(c):
                    ck = slice(128 * c, 128 * c + 128)
                    P.op("pe", (lambda ck=ck: (lambda e: e.matmul(pm[:, 0:128], lhsT=bTt[:, ck], rhs=cTt[:, ck], start=True, stop=True)))(), [bTt_r, cTt_r], [pm_r])
                    cbm, cbm_r = CBm[c % 3]
                    outs = []
                    tt("dve", cbm[:, :], pm[:, 0:128], U_f, ALU.mult, [pm_r, const_r], [cbm_r])
                    for qb in range(2):
                        k = 2 * c + qb
                        et, et_r = pair[0][:, 512 * qb:512 * qb + 512], pair_r[0][qb]
                        xtile, xtile_r = pair[1][:, 512 * qb:512 * qb + 512], pair_r[1][qb]
                        h0 = 8 * g + 4 * qb
                        selblk = selb[0:64, 128 * h0:128 * h0 + 512]
                        rt_, rt_r = Rt[k % 6]
                        rflat = rt_[:, :, :].rearrange("p a l -> p (a l)")
                        mm(et, [(ident_b, neg4[:, :, :].rearrange("p a l -> p (a l)")), (ones_b[0:64, :], rflat), (rowS[0:64, ck], selblk)],
                           [selb_r, hl_r, const_r, neg4_r, rt_r], [et_r])
                        mm(xtile, [(ones_b[0:64, :], rflat)], [rt_r, const_r], [xtile_r])
                        es_, es_r = Esb[k % 4]
                        ea_, ea_r = EAsb[k % 4]
                        mp_, mp_r = Mp[k % 4]
                        ce_, ce_r = CE[k % 4]
                        act(es_[:, :], et, AF.Exp, [et_r], [es_r])
                        act(ea_[:, :], xtile, AF.Exp, [xtile_r], [ea_r])
                        outs.append((mp_, mp_r, ce_, ce_r, es_, es_r, ea_, ea_r, cbm, cbm_r, ck))
                    return outs

                def RT(c):
                    ck = slice(128 * c, 128 * c + 128)
                    for qb in range(2):
                        k = 2 * c + qb
                        h0 = 8 * g + 4 * qb
                        selblk = selb[0:64, 128 * h0:128 * h0 + 512]
                        rt_, rt_r = Rt[k % 6]
                        tt("dve", rt_[:, :, :], selblk.rearrange("p (a l) -> p a l", a=4), acsS[0:64, ck].unsqueeze(1).to_broadcast([64, 4, 128]), ALU.mult, [selb_r, hl_r], [rt_r])

                def EXb(outs):
                    res = []
                    for (mp_, mp_r, ce_, ce_r, es_, es_r, ea_, ea_r, cbm, cbm_r, ck) in outs:
                        tt("dve", mp_[:, :, :], es_[:, :].rearrange("p (a l) -> p a l", a=4), cbm[:, :].unsqueeze(1).to_broadcast([128, 4, 128]), ALU.mult, [es_r, cbm_r], [mp_r])
                        tt("pool", ce_[:, :, :], ea_[:, :].rearrange("p (a l) -> p a l", a=4), cTt[:, ck].unsqueeze(1).to_broadcast([128, 4, 128]), ALU.mult, [ea_r, cTt_r], [ce_r])
                        res.append((mp_, mp_r, ce_, ce_r))
                    return res

                def XW(c):
                    xw, xw_r = xw2[c % 3]
                    tt("pool", v8(xw[:, :]), v8(xt[:, c, :]),
                       w2s[:, 32 * c + 8 * g:32 * c + 8 * g + 8].unsqueeze(2).to_broadcast([128, 8, 64]), ALU.mult, [xt_r, ssm_r], [xw_r])

                def ST(c):
                    xw, xw_r = xw2[c % 3]
                    mm(stp, [(bt[:, c, :], xw[:, :])], [bt_r, xw_r], [stp_r])

                XW(0)
                RT(0)
                RT(1)
                nxt = EXb(EXa(0))
                XW(1)
                tt("dve", v8(Sd[:, :]), v8(S[:, :]), decs[:, 8 * g:8 * g + 8].unsqueeze(2).to_broadcast([128, 8, 64]), ALU.mult, [S_r, ssm_r], [Sd_r])
                for c in range(NCH):
                    ck = slice(128 * c, 128 * c + 128)
                    cur = nxt
                    if c + 1 < NCH:
                        ST(c)
                        sbn, sbn_r = Sbs[(c + 1) % 2]
                        tt("dve", sbn[:, :], Sd[:, :], stp, ALU.add, [Sd_r, stp_r], [sbn_r])
                        tt("dve", S[:, :], Sd[:, :], stp, ALU.add, [Sd_r, stp_r], [S_r])
                        tt("pool", v8(Sd[:, :]), v8(S[:, :]), decs[:, 32 * (c + 1) + 8 * g:32 * (c + 1) + 8 * g + 8].unsqueeze(2).to_broadcast([128, 8, 64]), ALU.mult, [S_r, ssm_r], [Sd_r])
                        if c + 2 < NCH:
                            RT(c + 2)
                            if c + 3 < NCH:
                                XW(c + 2)
                        nxt = EXb(EXa(c + 1))
                    sbc, sbc_r = Sbs[c % 2]
                    for qb in range(2):
                        mp_, mp_r, ce_, ce_r = cur[qb]
                        for hh in range(4):
                            hl = 4 * qb + hh
                            h = 8 * g + hl
                            i, e2 = hl // 2, hl % 2
                            xh = xt[:, c, 64 * hl:64 * hl + 64]
                            mm(ybank[64 * e2:64 * e2 + 64, 128 * i:128 * i + 128],
                               [(xh, mp_[:, hh, :]), (sbc[:, 64 * hl:64 * hl + 64], ce_[:, hh, :]), (xh, dib[:, 128 * h:128 * h + 128])],
                               [xt_r, mp_r, sbc_r, ce_r, dib_r], [yb_r])
                    tt("dve", yz4[:, :, ck], ybank.rearrange("p (i l) -> p i l", i=4), sz4[:, :, ck], ALU.mult, [yb_r, sz4_r], [yz4_r])
                if g + 1 < 4:
                    zproj(g + 1)
                for i in range(4):
                    sq, sq_r = sqz[0]
                    act(sq[:, :], yz4[:, i, :], AF.Square, [yz4_r], [sq_r])
                    for blk in range(4):
                        o = pair[blk // 2][:, 512 * (blk % 2):512 * (blk % 2) + 512]
                        P.op("pe", (lambda o=o, sq=sq, blk=blk, i=i: (lambda e: e.matmul(o, lhsT=ones_b, rhs=sq[:, 512 * blk:512 * blk + 512], start=(i == 0), stop=(i == 3))))(),
                             [sq_r, const_r], [pair_r[blk // 2][blk % 2]])
                for pi in range(2):
                    ts("dve", rstdg[:, 1024 * pi:1024 * pi + 1024], pair[pi][:, :], 1.0 / 512.0, EPS, ALU.mult, ALU.add, [pair_r[pi]], [rstdg_r])
                act(rstdg[:, :], rstdg[:, :], AF.Ln, [rstdg_r], [rstdg_r])
                act(rstdg[:, :], rstdg[:, :], AF.Exp, [rstdg_r], [rstdg_r], scale=-0.5)
                for i in range(4):
                    y_, y_r = yo[0]
                    stt(y_[:, :], yz4[:, i, :], pp[:, PP_SNG + 4 * g + i:PP_SNG + 4 * g + i + 1], rstdg[:, :], ALU.mult, ALU.mult, [yz4_r, rstdg_r, const_r], [y_r])
                    kt = 4 * g + i
                    P.dma("pool", yssd_d[128 * kt:128 * kt + 128, :], y_[:, :], reads=[y_r], writes=[yssd_dr[kt]])

        if "merge" not in STOP:
          with Phase("merge") as ph:
            ys, ys_r = ph.tile("ys", [128, 16, 1024], BF16)
            yl, yl_r = ph.tile("yl", [128, 12, 1024], BF16)
            ym, ym_r = ph.tile("ym", [128, 8, 1024], BF16)
            mT, mT_r = ph.tile("mT", [128, 8, 1024], BF16)
            gs = [ph.tile(f"gs{i}", [128, 1024], F32) for i in range(3)]
            macc, macc_r = ph.tile("macc", [128, 1024], F32)
            mtmp, mtmp_r = ph.tile("mtmp", [128, 1024], F32)
            xtk = [ph.tile(f"xtk{i}", [128, 1024], F32) for i in range(2)]
            rs = [ph.tile(f"rs{i}", [128, 1024], F32) for i in range(2)]
            ssq, ssq_r = ph.tile("ssq", [128, 16], F32)
            ot = [ph.tile(f"ot{i}", [128, 1024], F32) for i in range(1)]
            w_out_v = w_out_d.rearrange("(j p) n -> p j n", p=128)
            ys_v = yssd_d.rearrange("(kt p) t -> p kt t", p=128)
            yl_v = ylru_d.rearrange("(kt p) t -> p kt t", p=128)
            ym_v = ymem_d.rearrange("(kt p) t -> p kt t", p=128)
            w_bs_v = w_bs_d.rearrange("(kt p) n -> p kt n", p=128)
            w_bl_v = w_bl_d.rearrange("(kt p) n -> p kt n", p=128)
            w_bm_v = w_bm_d.rearrange("(kt p) n -> p kt n", p=128)
            P.op("dve", lambda e: e.memset(ssq[:, :], 0.0), [], [ssq_r])
            ys_rk = [Res() for _ in range(16)]
            yl_rk = [Res() for _ in range(12)]
            ym_rk = [Res() for _ in range(8)]
            ssq_rk = [Res() for _ in range(16)]

            def load_lm(tb):
                tsl = slice(1024 * tb, 1024 * tb + 1024)
                for kt in range(12):
                    P.dma("sp", yl[:, kt, :], yl_v[:, kt, tsl], reads=[ylru_dr[kt]], writes=[yl_rk[kt]])
                for kt in range(8):
                    P.dma("sp", ym[:, kt, :], ym_v[:, kt, tsl], reads=[ymem_dr[kt]], writes=[ym_rk[kt]])

            def load_s(tb):
                tsl = slice(1024 * tb, 1024 * tb + 1024)
                for kt in range(16):
                    P.dma("sp", ys[:, kt, :], ys_v[:, kt, tsl], reads=[yssd_dr[kt]], writes=[ys_rk[kt]])

            load_lm(0)
            load_s(0)
            for tb in range(2):
                tsl = slice(1024 * tb, 1024 * tb + 1024)
                for j in range(8):
                    branches = [(ys, ys_rk, w_bs_v, 16), (yl, yl_rk, w_bl_v, 12), (ym, ym_rk, w_bm_v, 8)]
                    gl = []
                    for b in range(3):
                        gw, gwr = win_tile(COL_G + 1024 * b + 128 * j, 128)
                        g_, g_r = gs[b]
                        pr, prr = inproj_half(gw, gwr, 128, tb)
                        act(g_[:, :], pr[:, :], AF.Sigmoid, [prr], [g_r])
                        gl.append((g_, g_r))
                    for b, (yy, yy_rk, wvw, nk) in enumerate(branches):
                        g_, g_r = gl[b]
                        prp, prp_r = next_pair()
                        k0 = 0
                        first = True
                        while k0 < nk:
                            n = min(8, nk - k0)
                            wv, wr = wload([(0, n, wvw[:, k0:k0 + n, 128 * j:128 * j + 128])], n, 128)
                            last = (k0 + n >= nk)
                            for blk in range(2):
                                mm(prp[:, 512 * blk:512 * blk + 512], [(wv[:, kk, :], yy[:, k0 + kk, 512 * blk:512 * blk + 512]) for kk in range(n)],
                                   [wr] + yy_rk[k0:k0 + n], [prp_r[blk]], start=first, stop=last)
                            first = False
                            k0 += n
                        if b == 0:
                            tt("dve", macc[:, :], prp[:, :], g_[:, :], ALU.mult, [prp_r, g_r], [macc_r])
                        else:
                            tt("dve", mtmp[:, :], prp[:, :], g_[:, :], ALU.mult, [prp_r, g_r], [mtmp_r])
                            if b == 1:
                                tt("pool", macc[:, :], macc[:, :], mtmp[:, :], ALU.add, [macc_r, mtmp_r], [macc_r])
                            else:
                                tt("pool", mT[:, j, :], macc[:, :], mtmp[:, :], ALU.add, [macc_r, mtmp_r], [mT_r])
                wo, wo_r = ys, ys_rk[0:8]
                for j in range(8):
                    s32, r32 = wst.next()
                    P.dma("sp", s32[:, 0:1024], w_out_v[:, j, :], writes=[r32])
                    copy("dve", wo[:, j, :], s32[:, 0:1024], [r32], [ys_rk[j]])
                if tb == 0:
                    load_lm(1)
                for t8 in range(8):
                    k = 8 * tb + t8
                    xk, xk_r = xtk[k % 2]
                    r_, r_r = rs[k % 2]
                    o_, o_r = ot[0]
                    P.dma("sp", xk[:, :], xtok_d[128 * k:128 * k + 128, :], writes=[xk_r])
                    pr, prr = next_pair()
                    for half in range(2):
                        mm(pr[:, 512 * half:512 * half + 512], [(mT[:, j, 128 * t8:128 * t8 + 128], wo[:, j, 512 * half:512 * half + 512]) for j in range(8)], [mT_r, wo_r], [prr])
                    tt("dve", r_[:, :], pr[:, :], xk[:, :], ALU.add, [prr, xk_r], [r_r])
                    kr = ssq_rk[k]
                    act(o_[:, :], r_[:, :], AF.Square, [r_r, ssq_r, kr], [o_r, kr], accum=ssq[:, k:k + 1])
                    ts("dve", ssq[:, k:k + 1], ssq[:, k:k + 1], 1.0 / D, EPS, ALU.mult, ALU.add, [kr], [kr])
                    act(ssq[:, k:k + 1], ssq[:, k:k + 1], AF.Sqrt, [kr], [kr])
                    P.op("dve", (lambda k=k: (lambda e: e.reciprocal(out=ssq[:, k:k + 1], in_=ssq[:, k:k + 1])))(), [kr], [kr])
                    stt(o_[:, :], r_[:, :], ssq[:, k:k + 1], pb[:, PB_FG:PB_FG + 1024], ALU.mult, ALU.mult, [r_r, kr, const_r], [o_r])
                    final_ops.append(P.dma("pool", out_d[128 * k:128 * k + 128, :], o_[:, :], reads=[o_r]))
                if tb == 0:
                    load_s(1)

        if DEBUG:
            final_ops.append(P.dma("pool", dbg_d[:, 0:64], pay[:, :], reads=[pay_r]))
            final_ops.append(P.dma("pool", dbg_d[:, 64:64 + 2048], hT[:, 0, 3:TH], reads=[hT_r]))
            final_ops.append(P.dma("pool", dbg_d[:, 2112:2112 + 512], w2s[:, :], reads=[ssm_r]))
            final_ops.append(P.dma("pool", dbg_d[:, 2624:2624 + 512], decs[:, :], reads=[ssm_r]))
            final_ops.append(P.dma("pool", dbg_d[:, 3136:3136 + 12], hin[:, :], reads=[hin_r]))
            final_ops.append(P.dma("pool", dbg_d[:, 3200:3200 + 176], coef[:, :, :].rearrange("p a b -> p (a b)"), reads=[coef_r]))
            final_ops.append(P.dma("pool", dbg_d[0:64, 4096:4096 + 2048], acsS[:, :], reads=[hl_r])) if False else None
        fr = Res("final")
        for o in final_ops:
            if o is not None:
                fr.r.append(o)
        P.op("sp", lambda e: None, [], [fr])

        P.emit(nc, block, sems, dsem)
    return nc


def _tile_pp(v):
    return np.ascontiguousarray(v.reshape(-1, 128).T)


def prep_inputs(inp):
    f = np.float32
    x, mem = inp["x"], inp["mem"]
    pp = np.zeros((128, PP_N), f)
    pp[:, PP_NORMG:PP_NORMG + 8] = _tile_pp(inp["norm_g"][0])
    pp[:, PP_MEMG:PP_MEMG + 8] = _tile_pp(inp["mem_norm_g"][0])
    scw, scb = inp["ssd_conv_w"][0], inp["ssd_conv_b"][0]
    for t in range(24):
        for k in range(4):
            pp[:, PP_SCONV + 5 * t + k] = scw[k, t * 128:(t + 1) * 128]
        pp[:, PP_SCONV + 5 * t + 4] = scb[t * 128:(t + 1) * 128]
    lcw, lcb = inp["lru_conv_w"][0], inp["lru_conv_b"][0]
    for t in range(12):
        for k in range(4):
            pp[:, PP_LCONV + 5 * t + k] = lcw[k, t * 128:(t + 1) * 128]
        pp[:, PP_LCONV + 5 * t + 4] = lcb[t * 128:(t + 1) * 128]
    pp[:, PP_LBA:PP_LBA + 12] = _tile_pp(inp["lru_b_a"][0].reshape(-1))
    pp[:, PP_LBX:PP_LBX + 12] = _tile_pp(inp["lru_b_x"][0].reshape(-1))
    pp[:, PP_LAM:PP_LAM + 12] = _tile_pp(inp["lru_lambda"][0])
    pp[:, PP_SNG:PP_SNG + 16] = _tile_pp(inp["ssd_norm_g"][0].reshape(-1))
    pp[0:32, PP_DTB] = inp["ssd_dt_bias"][0]
    pp[32:64, PP_DTB] = inp["ssd_dt_bias"][0]
    pp[0:32, PP_ALOG] = inp["ssd_a_log"][0]
    pb = np.zeros((128, PB_N), f)
    pb[:, PB_DTB:PB_DTB + 32] = inp["ssd_dt_bias"][0][None, :]
    pb[:, PB_ALOG:PB_ALOG + 32] = inp["ssd_a_log"][0][None, :]
    pb[:, PB_FG:PB_FG + 1024] = inp["final_g"][None, :]
    cst = np.zeros((128, C_N), f)
    cst[:, C_ID:C_ID + 128] = np.eye(128, dtype=f)
    cst[:, C_U:C_U + 128] = np.triu(np.ones((128, 128), f))
    cst[:, C_ONE:C_ONE + 128] = 1.0
    cst[:, C_NEG:C_NEG + 128] = np.tril(np.full((128, 128), -32768.0, f), -1)
    sel2 = np.zeros((64, 32, 128), f)
    for h in range(32):
        sel2[h, h, :] = 1.0
        sel2[32 + h, h, :] = 1.0
    sel2 = sel2.reshape(64, 32 * 128)
    dih = np.zeros((128, 32, 128), f)
    dd = inp["ssd_d"][0]
    for h in range(32):
        dih[np.arange(128), h, np.arange(128)] = dd[h]
    dih = dih.reshape(128, 32 * 128)

    def bd(w):
        m = np.zeros((1536, 1536), f)
        for n in range(16):
            m[n * 96:(n + 1) * 96, n * 96:(n + 1) * 96] = w[n]
        return m
    wa_bd, wx_bd = bd(inp["lru_w_a"][0]), bd(inp["lru_w_x"][0])
    shared = {"w_in": np.ascontiguousarray(inp["w_in"][0]), "w_kv": np.ascontiguousarray(inp["w_kv"][0]),
              "w_br_ssd": np.ascontiguousarray(inp["w_br_ssd"][0]), "w_br_lru": np.ascontiguousarray(inp["w_br_lru"][0]),
              "w_br_mem": np.ascontiguousarray(inp["w_br_mem"][0]), "w_out": np.ascontiguousarray(inp["w_out"][0]),
              "lru_wa_bd": wa_bd, "lru_wx_bd": wx_bd, "pp": pp, "pb": pb, "cst": cst, "sel2": sel2, "dih": dih}
    maps = []
    for c in range(8):
        b, q = c // 4, c % 4
        t0 = q * T
        xs = np.zeros((TH, D), f)
        if q == 0:
            xs[3:] = x[b, 0:T]
        else:
            xs[:] = x[b, t0 - 3:t0 + T]
        xT = np.ascontiguousarray(xs.T.reshape(8, 128, TH).transpose(1, 0, 2))
        memT = np.ascontiguousarray(mem[b].T.reshape(8, 128, 256).transpose(1, 0, 2))
        exsel = np.zeros((128, 20), f)
        for j in range(4):
            exsel[:, j] = 1.0 if j < q else 0.0
            for m in range(4):
                exsel[:, 4 + 4 * j + m] = 1.0 if (j < m < q) else 0.0
        d = dict(shared)
        d.update({"xT": xT, "x_tok": np.ascontiguousarray(x[b, t0:t0 + T]), "memT": memT, "exsel": exsel})
        maps.append(d)
    return maps


_NC = None


def kernel(**inputs):
    global _NC
    inp = {k: np.asarray(v) for k, v in inputs.items()}
    maps = prep_inputs(inp)
    if _NC is None:
        _NC = build_nc()
    res = run_bass_kernel_spmd(_NC, maps, core_ids=list(range(8)))
    out = np.zeros((2, 4 * T, D), np.float32)
    for c in range(8):
        out[c // 4, (c % 4) * T:(c % 4 + 1) * T] = res.results[c]["out"]
    kernel.last = res
    return out
```

```python
import os
import numpy as np
import concourse.bass as bass
import concourse.mybir as mybir
from concourse.bass_utils import run_bass_kernel_spmd
from contextlib import ExitStack

F32 = mybir.dt.float32
BF16 = mybir.dt.bfloat16
AF = mybir.ActivationFunctionType
ALU = mybir.AluOpType
AX = mybir.AxisListType

D = 1024
T = 2048
TH = T + 3
NCH = 16
COL_Z, COL_X, COL_B, COL_C, COL_DT, COL_LG, COL_LX, COL_Q, COL_G = 0, 2048, 4096, 4608, 5120, 5152, 6688, 8224, 9248
IN_W = 12320
NPAY = 2048 + 32 + 12 + 12
EPS = 1e-6
DEBUG = int(os.environ.get("MK_DEBUG", "0"))
STOP = os.environ.get("MK_STOP", "")

PP_NORMG = 0
PP_MEMG = 8
PP_SCONV = 16
PP_LCONV = 136
PP_LBA = 196
PP_LBX = 208
PP_LAM = 220
PP_SNG = 232
PP_DTB = 248
PP_ALOG = 249
PP_N = 250
PB_DTB = 0
PB_ALOG = 32
PB_FG = 64
PB_N = 1088
C_ID = 0
C_U = 128
C_ONE = 256
C_NEG = 384
C_N = 512


class Res:
    __slots__ = ("name", "w", "r")

    def __init__(self, name=""):
        self.name = name
        self.w = None
        self.r = []


class Op:
    __slots__ = ("eng", "fn", "deps", "flag", "val", "is_dma", "slot", "sem")


class Prog:
    ENG = ("pe", "act", "dve", "pool", "sp")
    K = 8

    def __init__(self):
        self.ops = {e: [] for e in self.ENG}
        self.ndma = {e: 0 for e in self.ENG}
        self.fdeps = []
        self.fpend = set()

    def fence(self):
        deps = []
        for e in self.ENG:
            ndm = 0
            got_c = False
            for o in reversed(self.ops[e]):
                if o.is_dma:
                    if ndm < self.K + 2:
                        deps.append(o)
                        ndm += 1
                elif not got_c:
                    deps.append(o)
                    got_c = True
                if got_c and ndm >= self.K + 2:
                    break
        self.fdeps = deps
        self.fpend = set(self.ENG)

    def op(self, eng, fn, reads=(), writes=(), dma=False):
        o = Op()
        o.eng, o.fn, o.is_dma, o.flag, o.val, o.sem = eng, fn, dma, False, 0, None

        def flat(xs):
            out = []
            for x in xs:
                if isinstance(x, (list, tuple)):
                    out.extend(flat(x))
                else:
                    out.append(x)
            return out
        reads, writes = flat(reads), flat(writes)
        deps = []
        for r in reads:
            if r.w is not None:
                deps.append(r.w)
        for w in writes:
            if w.w is not None:
                deps.append(w.w)
            deps.extend(w.r)
        if eng in self.fpend:
            self.fpend.discard(eng)
            deps.extend(self.fdeps)
        dd, seen = [], set()
        for d in deps:
            if id(d) in seen or d is o:
                continue
            seen.add(id(d))
            if (not d.is_dma) and (not dma) and d.eng == "pe" and eng == "pe":
                continue
            dd.append(d)
            if not d.is_dma:
                d.flag = True
        o.deps = dd
        if dma:
            o.slot = self.ndma[eng]
            self.ndma[eng] += 1
        for r in reads:
            if not dma:
                r.r = [x for x in r.r if x.is_dma or x.eng != eng]
            r.r.append(o)
        for w in writes:
            w.w = o
            w.r = []
        self.ops[eng].append(o)
        return o

    def custom(self, eng, fn, sem, reads=(), writes=()):
        o = self.op(eng, fn, reads, writes, dma=True)
        self.ndma[eng] -= 1
        o.slot = -1
        o.sem = sem
        return o

    def dma(self, eng, out, in_, reads=(), writes=()):
        return self.op(eng, lambda e: e.dma_start(out=out, in_=in_), reads, writes, dma=True)

    def emit(self, nc, block, esem, dsem):
        for e in self.ENG:
            n = 0
            for o in self.ops[e]:
                if o.is_dma and o.slot < 0:
                    o.val = 1
                elif o.is_dma:
                    o.sem = dsem[e][o.slot % self.K]
                    o.val = 16 * (o.slot // self.K + 1)
                elif o.flag:
                    n += 1
                    o.val = n
                    o.sem = esem[e]

        def run(ename):
            def body(eng):
                seen = {}

                def wait(sem, val):
                    if seen.get(sem, 0) < val:
                        eng.wait_ge(sem, val)
                        seen[sem] = val

                for o in self.ops[ename]:
                    for d in o.deps:
                        wait(d.sem, d.val)
                    if o.is_dma and o.slot >= self.K:
                        wait(o.sem, o.val - 16)
                    ins = o.fn(eng)
                    if o.is_dma and o.slot < 0:
                        ins.then_inc(o.sem)
                    elif o.is_dma:
                        ins.then_inc(o.sem, 16)
                    elif o.flag:
                        ins.then_inc(o.sem, 1)
            return body

        block.tensor(run("pe"))
        block.scalar(run("act"))
        block.vector(run("dve"))
        block.gpsimd(run("pool"))
        block.sync(run("sp"))


class Ring:
    def __init__(self, tiles):
        self.t = tiles
        self.r = [Res() for _ in tiles]
        self.i = 0

    def next(self):
        k = self.i % len(self.t)
        self.i += 1
        return self.t[k], self.r[k]


def build_nc():
    nc = bass.Bass("TRN2", target_bir_lowering=False)
    P = Prog()

    def din(name, shape, dt=F32):
        return nc.dram_tensor(name, shape, dt, kind="ExternalInput").ap()

    xT_d = din("xT", [128, 8, TH])
    xtok_d = din("x_tok", [T, D])
    memT_d = din("memT", [128, 8, 256])
    w_in_d = din("w_in", [D, IN_W])
    w_kv_d = din("w_kv", [D, 2048])
    w_bs_d = din("w_br_ssd", [2048, D])
    w_bl_d = din("w_br_lru", [1536, D])
    w_bm_d = din("w_br_mem", [D, D])
    w_out_d = din("w_out", [D, D])
    wa_d = din("lru_wa_bd", [1536, 1536])
    wx_d = din("lru_wx_bd", [1536, 1536])
    pp_d = din("pp", [128, PP_N])
    pb_d = din("pb", [128, PB_N])
    cst_d = din("cst", [128, C_N])
    sel_d = din("sel2", [64, 32 * 128])
    di_d = din("dih", [128, 32 * 128])
    exs_d = din("exsel", [128, 20])
    out_d = nc.dram_tensor("out", [T, D], F32, kind="ExternalOutput").ap()

    def scratch(name, shape, dt):
        if DEBUG:
            return nc.dram_tensor(name, shape, dt, kind="ExternalOutput").ap()
        return nc.dram_tensor(name, shape, dt).ap()

    ylru_d = scratch("s_ylru", [1536, T], BF16)
    ymem_d = scratch("s_ymem", [D, T], BF16)
    yssd_d = scratch("s_yssd", [2048, T], BF16)
    la_d = scratch("s_la", [1536, T], BF16)
    u_d = scratch("s_u", [1536, T], BF16)
    xtm_d = scratch("s_xtm", [4, 128, NCH * 512], BF16)
    btm_d = scratch("s_btm", [4, 128, NCH * 128], BF16)
    bT_d = scratch("s_bT", [4, 128, T], BF16)
    cT_d = scratch("s_cT", [4, 128, T], BF16)
    cc_src = nc.dram_tensor("cc_src", [128, 2048], F32)
    cc_dst = nc.dram_tensor("cc_dst", [4 * 128, 2048], F32)
    cc_src_s = nc.dram_tensor("cc_src_s", [128, 64], F32)
    cc_dst_s = nc.dram_tensor("cc_dst_s", [4 * 128, 64], F32)
    dbg_d = scratch("s_dbg", [128, 8192], F32) if DEBUG else None
    ylru_dr = [Res() for _ in range(12)]
    ymem_dr = [Res() for _ in range(8)]
    yssd_dr = [Res() for _ in range(16)]
    la_dr = [Res() for _ in range(12)]
    u_dr = [Res() for _ in range(12)]
    grp_dr = [Res() for _ in range(4)]
    ccsrc_r = Res("ccsrc")
    ccdst_r = Res("ccdst")
    ccsrc_s_r = Res("ccsrc_s")
    ccdst_s_r = Res("ccdst_s")

    es = ExitStack()
    uid = [0]
    with es:
        def sb(name, shape, dt, st=None):
            uid[0] += 1
            return (st or es).enter_context(nc.sbuf_tensor(f"sb{uid[0]}_{name}", shape, dt))

        def ps(name, shape, dt):
            return es.enter_context(nc.psum_tensor("ps_" + name, shape, dt))

        class Phase:
            def __init__(self, name):
                self.name = name
                self.st = ExitStack()

            def __enter__(self):
                self.st.__enter__()
                return self

            def __exit__(self, *a):
                self.st.__exit__(*a)
                P.fence()
                return False

            def tile(self, name, shape, dt):
                return sb(name, shape, dt, self.st), Res(name)

        hT = sb("hT", [128, 8, TH], BF16)
        hT_r = Res("hT")
        pp = sb("pp", [128, PP_N], F32)
        pb = sb("pb", [128, PB_N], F32)
        cst = sb("cst", [128, C_N], F32)
        cstb = sb("cstb", [128, C_N], BF16)
        exs = sb("exs", [128, 20], F32)
        const_r = Res("const")
        ca_lru = sb("ca_lru", [128, 12], F32)
        small_r = Res("small")
        pay = sb("pay", [128, 64], F32)
        pay_r = Res("pay")
        lsum = sb("lsum", [128, 24], F32)
        lsum_r = Res("lsum")
        gat = sb("gat", [128, 4, 56], F32)
        gat_r = Res("gat")
        coef = sb("coef", [128, 4, 44], F32)
        coef_r = Res("coef")
        hin = sb("hin", [128, 12], F32)
        hin_r = Res("hin")
        w2s = sb("w2s", [128, 512], F32)
        decs = sb("decs", [128, 512], F32)
        ssm_r = Res("ssm")
        acsS = sb("acsS", [64, T], BF16)
        rowS = sb("rowS", [64, T], BF16)
        hl_r = Res("hilo")
        NW = 4
        wst = Ring([sb(f"wst{i}", [128, 1024], F32) for i in range(NW)])
        wbf = Ring([sb(f"wbf{i}", [128, 1024], BF16) for i in range(NW)])
        pair = [ps(f"pair{i}", [128, 1024], F32) for i in range(3)]
        pair_r = [[Res(f"pair{i}a"), Res(f"pair{i}b")] for i in range(3)]
        pbb = ps("pbb", [128, 1024], BF16)
        pbb_r = Res("pbb")
        pm = ps("pm", [128, 512], F32)
        pm_r = Res("pm")
        pair_i = [0]

        def next_pair():
            k = pair_i[0] % 3
            pair_i[0] += 1
            return pair[k], pair_r[k]

        sems = {e: es.enter_context(nc.semaphore(f"e_{e}")) for e in Prog.ENG}
        dsem = {e: [es.enter_context(nc.semaphore(f"d_{e}{k}")) for k in range(Prog.K)] for e in ("sp", "pool", "act")}
        cc_sem = es.enter_context(nc.semaphore("cc"))
        cc_sem_s = es.enter_context(nc.semaphore("ccs"))
        block = es.enter_context(nc.Block())

        def act(out, in_, func, reads, writes, bias=None, scale=None, accum=None):
            kw = {}
            if bias is not None:
                kw["bias"] = bias
            if scale is not None:
                kw["scale"] = scale
            if accum is not None:
                kw["accum_out"] = accum
            return P.op("act", lambda e: e.activation(out=out, in_=in_, func=func, **kw), reads, writes)

        def tt(eng, out, in0, in1, op, reads, writes):
            return P.op(eng, lambda e: e.tensor_tensor(out=out, in0=in0, in1=in1, op=op), reads, writes)

        def ts(eng, out, in0, s1, s2, op0, op1, reads, writes):
            if op1 is None:
                return P.op(eng, lambda e: e.tensor_scalar(out=out, in0=in0, scalar1=s1, scalar2=None, op0=op0), reads, writes)
            return P.op(eng, lambda e: e.tensor_scalar(out=out, in0=in0, scalar1=s1, scalar2=s2, op0=op0, op1=op1), reads, writes)

        def stt(out, in0, scalar, in1, op0, op1, reads, writes):
            return P.op("dve", lambda e: e.scalar_tensor_tensor(out=out, in0=in0, scalar=scalar, in1=in1, op0=op0, op1=op1), reads, writes)

        def copy(eng, out, in_, reads, writes):
            if eng == "act":
                return P.op("act", lambda e: e.copy(out=out, in_=in_), reads, writes)
            return P.op(eng, lambda e: e.tensor_copy(out=out, in_=in_), reads, writes)

        def mm(out, pairs, reads, writes, start=True, stop=True):
            def fn(e):
                n = len(pairs)
                ins = None
                for i, (l, r) in enumerate(pairs):
                    ins = e.matmul(out, lhsT=l, rhs=r, start=(start and i == 0), stop=(stop and i == n - 1))
                return ins
            return P.op("pe", fn, reads, writes)

        def transp(out, in_, ident, reads, writes):
            return P.op("pe", lambda e: e.transpose(out, in_, ident), reads, writes)

        cast_default = ["dve"]

        def wload(srcs, a, b, cast_eng=None, q="sp"):
            cast_eng = cast_eng or cast_default[0]
            s32, r32 = wst.next()
            s16, r16 = wbf.next()
            v32 = s32[:, 0:a * b].rearrange("p (a b) -> p a b", a=a)
            v16 = s16[:, 0:a * b].rearrange("p (a b) -> p a b", a=a)
            for (a0, a1, src) in srcs:
                P.dma(q, v32[:, a0:a1, :], src, writes=[r32])
            copy(cast_eng, v16, v32, [r32], [r16])
            return v16, r16

        w_in_v = w_in_d.rearrange("(kc p) n -> p kc n", p=128)

        def win_tile(c0, ncol):
            return wload([(0, 8, w_in_v[:, :, c0:c0 + ncol])], 8, ncol)

        def inproj_half(wv, wr, ncol, hf, dst=None):
            pr, prr = dst if dst is not None else next_pair()
            t0 = 3 + 1024 * hf
            for blk in range(2):
                pairs = [(wv[:, kc, 0:ncol], hT[:, kc, t0 + 512 * blk: t0 + 512 * blk + 512]) for kc in range(8)]
                mm(pr[0:ncol, 512 * blk: 512 * blk + 512], pairs, [wr, hT_r], [prr[blk]])
            return pr, prr

        def inproj_halo(wv, wr, ncol):
            pairs = [(wv[:, kc, 0:ncol], hT[:, kc, 0:3]) for kc in range(8)]
            mm(pm[0:ncol, 0:3], pairs, [wr, hT_r], [pm_r])

        def conv_A(wvr, wcol, out_f32, out_r, xr, xrr):
            wv, wr = wvr
            inproj_halo(wv, wr, 128)
            copy("act", xr[:, 0:3], pm[:, 0:3], [pm_r], [xrr])
            for hf in range(2):
                pr, prr = inproj_half(wv, wr, 128, hf)
                copy("act", xr[:, 3 + 1024 * hf: 3 + 1024 * hf + 1024], pr[:, :], [prr], [xrr])
                act(out_f32[:, 1024 * hf: 1024 * hf + 1024], pr[:, :], AF.Identity, [prr, const_r], [out_r[hf] if isinstance(out_r, list) else out_r],
                    bias=pp[:, wcol + 4: wcol + 5], scale=pp[:, wcol + 3: wcol + 4])

        def conv_B(wcol, out_f32, out_r, xr, xrr):
            w = lambda k: pp[:, wcol + k: wcol + k + 1]
            for hf in range(2):
                o = out_f32[:, 1024 * hf: 1024 * hf + 1024]
                b0 = 1024 * hf
                orr = out_r[hf] if isinstance(out_r, list) else out_r
                for k in (2, 1, 0):
                    stt(o, xr[:, b0 + k: b0 + k + 1024], w(k), o, ALU.mult, ALU.add, [xrr, orr, const_r], [orr])

        final_ops = []

        P.dma("sp", pp[:], pp_d, writes=[const_r])
        P.dma("sp", pb[:], pb_d, writes=[const_r])
        P.dma("sp", cst[:], cst_d, writes=[const_r])
        P.dma("sp", exs[:], exs_d, writes=[const_r])
        copy("pool", cstb[:], cst[:], [const_r], [const_r])
        P.op("dve", lambda e: e.memset(pay[:, :], 0.0), [], [pay_r])
        ident_b = cstb[:, C_ID:C_ID + 128]
        ones_b = cstb[:, C_ONE:C_ONE + 128]
        U_f = cst[:, C_U:C_U + 128]
        ones_f = cst[:, C_ONE:C_ONE + 128]
        neg_b = cstb[:, C_NEG:C_NEG + 128]
        blocks5 = [(0, 512), (512, 512), (1024, 512), (1536, 512), (2048, 3)]

        def ps_blk(bi):
            if bi < 2:
                return pair[0][:, 512 * bi: 512 * bi + 512], pair_r[0][bi]
            if bi < 4:
                return pair[1][:, 512 * (bi - 2): 512 * (bi - 2) + 512], pair_r[1][bi - 2]
            return pm[:, 0:3], pm_r

        def rms_featmajor(ph, src_d, ncols, gcol, dst, dst_r, blocks):
            xs = [ph.tile(f"x{kc}", [128, ncols], F32) for kc in range(8)]
            sqs = [ph.tile(f"sq{i}", [128, ncols], BF16) for i in range(2)]
            rstd, rstd_r = ph.tile("rstd", [128, ncols], F32)
            for kc in range(8):
                P.dma("sp", xs[kc][0][:, :], src_d[:, kc, :], writes=[xs[kc][1]])
            for kc in range(8):
                sq, sqr = sqs[kc % 2]
                act(sq[:, :], xs[kc][0][:, :], AF.Square, [xs[kc][1]], [sqr])
                for (o, orr, c0, n) in blocks:
                    P.op("pe", (lambda o=o, sq=sq, c0=c0, n=n, kc=kc: (lambda e: e.matmul(o[:, 0:n], lhsT=ones_b, rhs=sq[:, c0:c0 + n], start=(kc == 0), stop=(kc == 7))))(),
                         [sqr, const_r], [orr])
            for (o, orr, c0, n) in blocks:
                ts("dve", rstd[:, c0:c0 + n], o[:, 0:n], 1.0 / D, EPS, ALU.mult, ALU.add, [orr], [rstd_r])
            act(rstd[:, :], rstd[:, :], AF.Ln, [rstd_r], [rstd_r])
            act(rstd[:, :], rstd[:, :], AF.Exp, [rstd_r], [rstd_r], scale=-0.5)
            for kc in range(8):
                stt(dst[:, kc, :], xs[kc][0][:, :], pp[:, gcol + kc: gcol + kc + 1], rstd[:, :],
                    ALU.mult, ALU.mult, [xs[kc][1], rstd_r, const_r], [dst_r])

        with Phase("p0") as ph:
            blks = []
            for bi, (c0, n) in enumerate(blocks5):
                o, orr = ps_blk(bi)
                blks.append((o, orr, c0, n))
            rms_featmajor(ph, xT_d, TH, PP_NORMG, hT, hT_r, blks)
            act(ca_lru[:], pp[:, PP_LAM:PP_LAM + 12], AF.Exp, [const_r], [small_r], scale=-1.0)
            act(ca_lru[:], ca_lru[:], AF.Ln, [small_r], [small_r], bias=1.0)
            ts("dve", ca_lru[:], ca_lru[:], -8.0, None, ALU.mult, None, [small_r], [small_r])

        def lru_tiles_for(jo):
            b0 = (128 * jo) // 96
            b1 = (128 * jo + 127) // 96
            return (96 * b0) // 128, (96 * (b1 + 1) - 1) // 128

        wa_v = wa_d.rearrange("(i p) n -> p i n", p=128)
        wx_v = wx_d.rearrange("(i p) n -> p i n", p=128)

        if "lru1" not in STOP:
          cast_default[0] = "act"
          with Phase("lru1") as ph:
            xl_ring = [(ph.tile(f"xl{i}", [128, T], F32)[0], [Res(), Res()]) for i in range(3)]
            xlb_ring = [(ph.tile(f"xlb{i}", [128, T], BF16)[0], [Res(), Res()]) for i in range(3)]
            xraw = [ph.tile(f"xraw{i}", [128, TH], F32) for i in range(2)]
            tR = [ph.tile(f"tR{i}", [128, 1024], F32) for i in range(3)]
            tI = [ph.tile(f"tI{i}", [128, 1024], F32) for i in range(3)]
            tA = [ph.tile(f"tA{i}", [128, 1024], F32) for i in range(3)]
            tE = [ph.tile(f"tE{i}", [128, 1024], F32) for i in range(3)]
            labs = [ph.tile(f"lab{i}", [128, T], BF16) for i in range(2)]
            ubs = [ph.tile(f"ub{i}", [128, T], BF16) for i in range(2)]
            hctr = [0]
            cah, cah_r = ph.tile("cah", [128, 12], F32)
            ts("dve", cah[:, :], ca_lru[:, :], 0.5, None, ALU.mult, None, [small_r], [cah_r])

            def lru_gates(jo):
                i0, i1 = lru_tiles_for(jo)
                ni = i1 - i0 + 1
                wav, war = wload([(0, ni, wa_v[:, i0:i1 + 1, jo * 128:(jo + 1) * 128])], ni, 128)
                wxv, wxr = wload([(0, ni, wx_v[:, i0:i1 + 1, jo * 128:(jo + 1) * 128])], ni, 128)
                xl, xlr = xl_ring[jo % 3]
                lab, lab_r = labs[jo % 2]
                ub, ub_r = ubs[jo % 2]
                prev_h = None
                st = []
                for hf in range(2):
                    tk = slice(1024 * hf, 1024 * hf + 1024)
                    k = hctr[0]
                    hctr[0] += 1
                    (R_, R_r), (I_, I_r), (A_, A_r), (E_, E_r) = tR[k % 3], tI[k % 3], tA[k % 3], tE[k % 3]
                    pr_r, pr_rr = next_pair()
                    pr_i, pr_ir = next_pair()
                    for blk in range(2):
                        cs = slice(1024 * hf + 512 * blk, 1024 * hf + 512 * blk + 512)
                        os_ = slice(512 * blk, 512 * blk + 512)
                        mm(pr_r[:, os_], [(wav[:, i - i0, :], xlb_ring[i % 3][0][:, cs]) for i in range(i0, i1 + 1)],
                           [war] + [xlb_ring[i % 3][1][hf] for i in range(i0, i1 + 1)], [pr_rr[blk]])
                        mm(pr_i[:, os_], [(wxv[:, i - i0, :], xlb_ring[i % 3][0][:, cs]) for i in range(i0, i1 + 1)],
                           [wxr] + [xlb_ring[i % 3][1][hf] for i in range(i0, i1 + 1)], [pr_ir[blk]])
                    act(R_[:, :], pr_r[:, :], AF.Tanh, [pr_rr, const_r], [R_r], bias=bh[:, jo:jo + 1], scale=0.5)
                    act(I_[:, :], pr_i[:, :], AF.Tanh, [pr_ir, const_r], [I_r], bias=bh[:, 12 + jo:13 + jo], scale=0.5)
                    ts("dve", lab[:, tk], R_[:, :], cah[:, jo:jo + 1], cah[:, jo:jo + 1], ALU.mult, ALU.add, [R_r, cah_r], [lab_r])
                    act(E_[:, :], lab[:, tk], AF.Exp, [lab_r], [E_r], scale=2.0)
                    act(A_[:, :], lab[:, tk], AF.Exp, [lab_r], [A_r])
                    stt(I_[:, :], I_[:, :], 1.0, xl[:, tk], ALU.add, ALU.mult, [I_r, xlr[hf]], [I_r])
                    ts("pool", E_[:, :], E_[:, :], -1.0, 1.0, ALU.mult, ALU.add, [E_r], [E_r])
                    st.append((tk, R_, R_r, I_, I_r, A_, A_r, E_, E_r))
                for hf, (tk, R_, R_r, I_, I_r, A_, A_r, E_, E_r) in enumerate(st):
                    act(E_[:, :], E_[:, :], AF.Sqrt, [E_r], [E_r])
                    stt(ub[:, tk], E_[:, :], 0.5, I_[:, :], ALU.mult, ALU.mult, [E_r, I_r], [ub_r])
                    init = 0.0 if hf == 0 else prev_h[:, 1023:1024]
                    rd = [A_r, ub_r, R_r] + ([st[0][2]] if hf == 1 else [])
                    P.op("dve", (lambda tk=tk, init=init, R_=R_, A_=A_, ub=ub: (lambda e: e.tensor_tensor_scan(out=R_[:, :], data0=A_[:, :], data1=ub[:, tk], initial=init, op0=ALU.mult, op1=ALU.add)))(),
                         rd, [R_r])
                    prev_h = R_
                    P.op("dve", (lambda tk=tk, hf=hf, lab=lab: (lambda e: e.reduce_sum(out=lsum[:, 2 * jo + hf: 2 * jo + hf + 1], in_=lab[:, tk], axis=AX.X)))(),
                         [lab_r], [lsum_r])
                copy("dve", pay[:, 44 + jo: 45 + jo], prev_h[:, 1023:1024], [st[1][2]], [pay_r])
                P.dma("pool", la_d[jo * 128:(jo + 1) * 128, :], lab[:, 0:T], reads=[lab_r], writes=[la_dr[jo]])
                P.dma("pool", u_d[jo * 128:(jo + 1) * 128, :], ub[:, 0:T], reads=[ub_r], writes=[u_dr[jo]])

            bh, bh_r = ph.tile("bh", [128, 24], F32)
            ts("dve", bh[:, :], pp[:, PP_LBA:PP_LBA + 24], 0.5, None, ALU.mult, None, [const_r], [const_r])
            lw = {}

            def lruW(j):
                if j < 12:
                    lw[j] = win_tile(COL_LX + 128 * j, 128)

            def lruA(j):
                conv_A(lw.pop(j), PP_LCONV + 5 * j, xl_ring[j % 3][0], xl_ring[j % 3][1], xraw[j % 2][0], xraw[j % 2][1])

            lruW(0)
            lruW(1)
            lruA(0)
            for j in range(12):
                xr, xrr = xraw[j % 2]
                xl, xlr = xl_ring[j % 3]
                xlb, xlbr = xlb_ring[j % 3]
                lruW(j + 2)
                if j + 1 < 12:
                    lruA(j + 1)
                conv_B(PP_LCONV + 5 * j, xl, xlr, xr, xrr)
                copy("act", xlb[:, 0:1024], xl[:, 0:1024], [xlr[0]], [xlbr[0]])
                copy("pool", xlb[:, 1024:T], xl[:, 1024:T], [xlr[1]], [xlbr[1]])
                if j >= 1:
                    lru_gates(j - 1)
            lru_gates(11)
            lv = lsum[:].rearrange("p (j two) -> p j two", two=2)
            tt("dve", pay[:, 32:44], lv[:, :, 0], lv[:, :, 1], ALU.add, [lsum_r], [pay_r])
          cast_default[0] = "dve"

        if "ssd1" not in STOP:
          cast_default[0] = "act"
          with Phase("ssd1") as ph:
            xraw = [ph.tile(f"xraw{i}", [128, TH], F32) for i in range(2)]
            acc = [ph.tile(f"acc{i}", [128, T], F32) for i in range(2)]
            fm = [ph.tile(f"fm{i}", [128, T], BF16) for i in range(4)]
            xtm = [ph.tile(f"xtm{i}", [128, NCH, 512], BF16) for i in range(2)]
            btm = [ph.tile(f"btm{i}", [128, NCH, 128], BF16) for i in range(2)]
            xw1 = [ph.tile(f"xw{i}", [128, 512], BF16) for i in range(3)]
            sloc = [ph.tile(f"sloc{i}", [128, 512], F32) for i in range(2)]
            dt_tm, dtm_r = ph.tile("dt_tm", [128, 512], F32)
            dtA_tm, dta_r = ph.tile("dtA_tm", [128, 512], F32)
            acs_tm, acs_r = ph.tile("acs_tm", [128, 512], F32)
            tot_bc, tot_r = ph.tile("tot_bc", [128, 512], F32)
            pre, pre_r = ph.tile("pre", [128, 512], F32)
            w1, w1_r = ph.tile("w1", [128, 512], F32)
            anegb, anegb_r = ph.tile("anegb", [128, 32], F32)
            anegp, anegp_r = ph.tile("anegp", [64, 1], F32)
            dtbp, dtbp_r = ph.tile("dtbp", [64, 1], F32)
            logd, logd_r = ph.tile("logd", [128, 32], F32)
            v3 = lambda t: t[:, :].rearrange("p (c h) -> p c h", c=NCH)

            wdv, wdr = wload([(0, 8, w_in_v[:, :, COL_DT:COL_DT + 32])], 8, 32)
            prd, prd_r = next_pair()
            for c in range(NCH):
                mm(prd[:, 32 * c:32 * c + 32], [(hT[:, kc, 3 + 128 * c: 3 + 128 * c + 128], wdv[:, kc, :]) for kc in range(8)], [wdr, hT_r], [prd_r])
            tt("dve", v3(dt_tm), prd[:, 0:512].rearrange("p (c h) -> p c h", c=NCH),
               pb[:, PB_DTB:PB_DTB + 32].unsqueeze(1).to_broadcast([128, NCH, 32]), ALU.add, [prd_r, const_r], [dtm_r])
            act(dt_tm[:, :], dt_tm[:, :], AF.Exp, [dtm_r], [dtm_r])
            act(dt_tm[:, :], dt_tm[:, :], AF.Ln, [dtm_r], [dtm_r], bias=1.0)
            act(anegb[:, :], pb[:, PB_ALOG:PB_ALOG + 32], AF.Exp, [const_r], [anegb_r])
            ts("dve", anegb[:, :], anegb[:, :], -1.0, None, ALU.mult, None, [anegb_r], [anegb_r])
            tt("dve", v3(dtA_tm), v3(dt_tm), anegb[:, :].unsqueeze(1).to_broadcast([128, NCH, 32]), ALU.mult, [dtm_r, anegb_r], [dta_r])
            pr2, pr2_r = next_pair()
            mm(pr2[:, 0:512], [(U_f, dtA_tm[:, :])], [dta_r, const_r], [pr2_r])
            mm(pr2[:, 512:1024], [(ones_f, dtA_tm[:, :])], [dta_r, const_r], [pr2_r])
            copy("act", acs_tm[:, :], pr2[:, 0:512], [pr2_r], [acs_r])
            copy("act", tot_bc[:, :], pr2[:, 512:1024], [pr2_r], [tot_r])
            tt("dve", w2s[:, :], tot_bc[:, :], acs_tm[:, :], ALU.subtract, [tot_r, acs_r], [ssm_r])
            act(w2s[:, :], w2s[:, :], AF.Exp, [ssm_r], [ssm_r])
            tt("dve", w2s[:, :], w2s[:, :], dt_tm[:, :], ALU.mult, [ssm_r, dtm_r], [ssm_r])
            act(decs[:, :], tot_bc[:, :], AF.Exp, [tot_r], [ssm_r])
            P.op("dve", lambda e: e.memset(pre[:, 0:32], 0.0), [], [pre_r])
            for c in range(1, NCH):
                tt("dve", pre[:, 32 * c:32 * c + 32], pre[:, 32 * (c - 1):32 * c], tot_bc[:, 32 * (c - 1):32 * c], ALU.add, [pre_r, tot_r], [pre_r])
            tt("dve", logd[:, :], pre[:, 480:512], tot_bc[:, 480:512], ALU.add, [pre_r, tot_r], [logd_r])
            copy("dve", pay[:, 0:32], logd[:, :], [logd_r], [pay_r])
            tt("dve", v3(w1), v3(pre), logd[:, :].unsqueeze(1).to_broadcast([128, NCH, 32]), ALU.subtract, [logd_r, pre_r], [w1_r])
            tt("dve", w1[:, :], w1[:, :], acs_tm[:, :], ALU.add, [w1_r, acs_r], [w1_r])
            act(w1[:, :], w1[:, :], AF.Exp, [w1_r], [w1_r], scale=-1.0)
            tt("dve", w1[:, :], w1[:, :], dt_tm[:, :], ALU.mult, [w1_r, dtm_r], [w1_r])

            f_dt, f_dt_r = xraw[0]
            f_ln, f_ln_r = xraw[1]
            f_ac, f_ac_r = acc[0]
            f_t, f_t_r = acc[1]
            for hf in range(2):
                pr, prr = next_pair()
                t0 = 3 + 1024 * hf
                for blk in range(2):
                    pairs = []
                    for kc in range(8):
                        pairs.append((wdv[:, kc, :], hT[:, kc, t0 + 512 * blk: t0 + 512 * blk + 512]))
                    mm(pr[0:32, 512 * blk:512 * blk + 512], pairs, [wdr, hT_r], [prr])
                    mm(pr[32:64, 512 * blk:512 * blk + 512], pairs, [wdr, hT_r], [prr])
                act(f_dt[0:64, 1024 * hf:1024 * hf + 1024], pr[0:64, :], AF.Exp, [prr, const_r], [f_dt_r], bias=pp[0:64, PP_DTB:PP_DTB + 1])
            act(f_dt[0:64, 0:T], f_dt[0:64, 0:T], AF.Ln, [f_dt_r], [f_dt_r], bias=1.0)
            act(f_ln[0:64, 0:T], f_dt[0:64, 0:T], AF.Ln, [f_dt_r], [f_ln_r])
            dtA2, dtA2_r = ph.tile("dtA2", [128, NCH, 64], F32)
            copy("pool", dtA2[:, :, 0:32], v3(dtA_tm), [dta_r], [dtA2_r])
            copy("pool", dtA2[:, :, 32:64], v3(dtA_tm), [dta_r], [dtA2_r])
            for q4 in range(4):
                pr, prr = next_pair()
                for cc in range(4):
                    c = 4 * q4 + cc
                    mm(pr[0:64, 128 * cc:128 * cc + 128], [(dtA2[:, c, :], U_f)], [dtA2_r, const_r], [prr])
                copy("act", f_ac[0:64, 512 * q4:512 * q4 + 512], pr[0:64, 0:512], [prr], [f_ac_r])
            tt("dve", f_ln[0:64, 0:T], f_ln[0:64, 0:T], f_ac[0:64, 0:T], ALU.subtract, [f_ln_r, f_ac_r], [f_ln_r])

            def hilo(src, src_r, dstS):
                copy("dve", dstS[0:32, :], src[0:32, 0:T], [src_r], [hl_r])
                hb, hb_r = fm[0]
                copy("dve", hb[32:64, 0:T], src[32:64, 0:T], [src_r], [hb_r])
                tt("dve", f_t[32:64, 0:T], src[32:64, 0:T], hb[32:64, 0:T], ALU.subtract, [src_r, hb_r], [f_t_r])
                copy("dve", dstS[32:64, :], f_t[32:64, 0:T], [f_t_r], [hl_r])
            hilo(f_ac, f_ac_r, acsS)
            hilo(f_ln, f_ln_r, rowS)

            tiles = []
            for g in range(4):
                tiles += [(g, COL_X + 512 * g + 128 * i, 4 * g + i, "x", i) for i in range(4)]
                tiles += [(g, COL_B + 128 * g, 16 + g, "b", 0), (g, COL_C + 128 * g, 20 + g, "c", 0)]
            pbb_h = [Res("pbbA"), Res("pbbB")]

            sw = {}

            def ssdW(k):
                if k < len(tiles):
                    sw[k] = win_tile(tiles[k][1], 128)

            def ssdA(k):
                g, c0, tix, kind, i = tiles[k]
                conv_A(sw.pop(k), PP_SCONV + 5 * tix, acc[k % 2][0], acc[k % 2][1], xraw[k % 2][0], xraw[k % 2][1])

            ssdW(0)
            ssdW(1)
            ssdA(0)
            tq = [0]
            for k, (g, c0, tix, kind, i) in enumerate(tiles):
                xt, xt_r = xtm[g % 2]
                bt, bt_r = btm[g % 2]
                xr, xrr = xraw[k % 2]
                ac, ac_r = acc[k % 2]
                f, f_r = fm[k % 4]
                ssdW(k + 2)
                if k + 1 < len(tiles):
                    ssdA(k + 1)
                conv_B(PP_SCONV + 5 * tix, ac, ac_r, xr, xrr)
                act(f[:, 0:T], ac[:, 0:T], AF.Silu, [ac_r], [f_r])
                if kind in ("x", "b"):
                    for half in range(2):
                        for cc in range(8):
                            c = 8 * half + cc
                            transp(pbb[:, 128 * cc:128 * cc + 128], f[:, 128 * c:128 * c + 128], ident_b, [f_r, const_r], [pbb_r])
                        src = pbb[:, 0:1024].rearrange("p (c n) -> p c n", c=8)
                        if kind == "x":
                            copy("act", xt[:, 8 * half:8 * half + 8, 128 * i:128 * i + 128], src, [pbb_r], [xt_r])
                        else:
                            copy("act", bt[:, 8 * half:8 * half + 8, :], src, [pbb_r], [bt_r])
                if kind == "b":
                    P.dma("pool", bT_d[g], f[:, 0:T], reads=[f_r], writes=[grp_dr[g]])
                if kind == "c":
                    P.dma("pool", cT_d[g], f[:, 0:T], reads=[f_r], writes=[grp_dr[g]])
                    prs, prs_r = next_pair()
                    for c in range(NCH):
                        xw, xw_r = xw1[c % 3]
                        tt("dve" if c % 2 == 0 else "pool", xw[:, :].rearrange("p (h q) -> p h q", h=8), xt[:, c, :].rearrange("p (h q) -> p h q", h=8),
                           w1[:, 32 * c + 8 * g:32 * c + 8 * g + 8].unsqueeze(2).to_broadcast([128, 8, 64]), ALU.mult, [xt_r, w1_r], [xw_r])
                        mm(prs[:, 0:512], [(bt[:, c, :], xw[:, :])], [bt_r, xw_r], [prs_r[0]], start=(c == 0), stop=(c == NCH - 1))
                    sl, sl_r = sloc[g % 2]
                    copy("act", sl[:, :], prs[:, 0:512], [prs_r[0]], [sl_r])
                    P.dma("pool", cc_src.ap()[:, 512 * g:512 * g + 512], sl[:, :], reads=[sl_r], writes=[ccsrc_r])
                    P.dma("pool", xtm_d[g], xt[:, :, :].rearrange("p c n -> p (c n)"), reads=[xt_r], writes=[grp_dr[g]])
                    P.dma("pool", btm_d[g], bt[:, :, :].rearrange("p c n -> p (c n)"), reads=[bt_r], writes=[grp_dr[g]])

        cast_default[0] = "dve"
        if "xchg" not in STOP:
            P.dma("pool", cc_src_s.ap()[:, :], pay[:, :], reads=[pay_r], writes=[ccsrc_s_r])
            P.custom("pool", lambda e: e.collective_compute("AllGather", ALU.bypass, replica_groups=[[0, 1, 2, 3], [4, 5, 6, 7]],
                                                            ins=[cc_src_s.ap().opt()], outs=[cc_dst_s.ap().opt()]),
                     cc_sem_s, reads=[ccsrc_s_r], writes=[ccdst_s_r])
            cc_op = P.custom("pool", lambda e: e.collective_compute("AllGather", ALU.bypass, replica_groups=[[0, 1, 2, 3], [4, 5, 6, 7]],
                                                                    ins=[cc_src.ap().opt()], outs=[cc_dst.ap().opt()]),
                             cc_sem, reads=[ccsrc_r], writes=[ccdst_r])

        w_kv_v = w_kv_d.rearrange("(kc p) n -> p kc n", p=128)
        if "attn" not in STOP:
          with Phase("attn") as ph:
            mnT, mnT_r = ph.tile("mnT", [128, 8, 256], BF16)
            kT, kT_r = ph.tile("kT", [128, 8, 256], BF16)
            Vt, V_r = ph.tile("V", [128, 2, 1024], BF16)
            qT = [ph.tile(f"qT{i}", [128, 2, T], BF16) for i in range(2)]
            PT = [ph.tile(f"PT{i}", [128, 2, 512], BF16) for i in range(2)]
            rinv = [ph.tile(f"rinv{i}", [128, 512], F32) for i in range(2)]
            yo = [ph.tile(f"yo{i}", [128, T], BF16) for i in range(2)]
            rms_featmajor(ph, memT_d, 256, PP_MEMG, mnT, mnT_r, [(pair[0][:, 0:256], pair_r[0], 0, 256)])
            for dtile in range(8):
                wv, wr = wload([(0, 8, w_kv_v[:, :, 128 * dtile:128 * dtile + 128])], 8, 128)
                if dtile % 4 == 0:
                    pr, prr = next_pair()
                o = pr[:, 256 * (dtile % 4):256 * (dtile % 4) + 256]
                mm(o, [(wv[:, kc, :], mnT[:, kc, :]) for kc in range(8)], [wr, mnT_r], [prr])
                if dtile % 4 == 3:
                    copy("act", kT[:, dtile - 3:dtile + 1, :], pr[:, :].rearrange("p (a m) -> p a m", a=4), [prr], [kT_r])
            for dvt in range(8):
                wv, wr = wload([(0, 8, w_kv_v[:, :, 1024 + 128 * dvt:1024 + 128 * dvt + 128])], 8, 128)
                if dvt % 4 == 0:
                    pr, prr = next_pair()
                for mt in range(2):
                    o = pr[:, 512 * mt + 128 * (dvt % 4):512 * mt + 128 * (dvt % 4) + 128]
                    mm(o, [(mnT[:, kc, 128 * mt:128 * mt + 128], wv[:, kc, :]) for kc in range(8)], [wr, mnT_r], [prr])
                if dvt % 4 == 3:
                    copy("act", Vt[:, :, 512 * (dvt // 4):512 * (dvt // 4) + 512], pr[:, :].rearrange("p (mt n) -> p mt n", mt=2), [prr], [V_r])
            for hd in range(4):
                q, q_r = qT[hd % 2]
                for dc in range(2):
                    wv, wr = win_tile(COL_Q + 256 * hd + 128 * dc, 128)
                    for hf in range(2):
                        pr, prr = inproj_half(wv, wr, 128, hf)
                        copy("act", q[:, dc, 1024 * hf:1024 * hf + 1024], pr[:, :], [prr], [q_r])
                y0, y0_r = yo[0]
                y1, y1_r = yo[1]
                for tb in range(4):
                    tk = slice(512 * tb, 512 * tb + 512)
                    pt, pt_r = PT[tb % 2]
                    prs, prs_r = next_pair()
                    for mt in range(2):
                        mm(prs[:, 512 * mt:512 * mt + 512], [(kT[:, 2 * hd + dc, 128 * mt:128 * mt + 128], q[:, dc, tk]) for dc in range(2)], [kT_r, q_r], [prs_r])
                    act(pt[:, :, :], prs[:, :].rearrange("p (mt n) -> p mt n", mt=2), AF.Exp, [prs_r], [pt_r], scale=1.0 / 16.0)
                    pro, pro_r = next_pair()
                    P.op("pe", (lambda pt=pt: (lambda e: e.matmul(pm[:, 0:512], lhsT=ones_b, rhs=pt[:, 0, :], start=True, stop=False)))(), [pt_r, const_r], [pm_r])
                    P.op("pe", (lambda pt=pt: (lambda e: e.matmul(pm[:, 0:512], lhsT=ones_b, rhs=pt[:, 1, :], start=False, stop=True)))(), [pt_r, const_r], [pm_r])
                    for dvt in range(2):
                        mm(pro[:, 512 * dvt:512 * dvt + 512], [(Vt[:, mt, 256 * hd + 128 * dvt:256 * hd + 128 * dvt + 128], pt[:, mt, :]) for mt in range(2)], [V_r, pt_r], [pro_r])
                    ri, ri_r = rinv[tb % 2]
                    P.op("dve", (lambda ri=ri: (lambda e: e.reciprocal(out=ri[:, :], in_=pm[:, 0:512])))(), [pm_r], [ri_r])
                    tt("dve", y0[:, tk], pro[:, 0:512], ri[:, :], ALU.mult, [pro_r, ri_r], [y0_r])
                    tt("dve", y1[:, tk], pro[:, 512:1024], ri[:, :], ALU.mult, [pro_r, ri_r], [y1_r])
                for dvt, (yy, yy_r) in enumerate(((y0, y0_r), (y1, y1_r))):
                    P.dma("pool", ymem_d[256 * hd + 128 * dvt:256 * hd + 128 * dvt + 128, :], yy[:, 0:T], reads=[yy_r], writes=[ymem_dr[2 * hd + dvt]])

        if "xchg" not in STOP:
            cc_v = cc_dst.ap().rearrange("(r p) f -> p r f", p=128)
            cc_vs = cc_dst_s.ap().rearrange("(r p) f -> p r f", p=128)
            P.dma("sp", gat[:, :, :], cc_vs[:, :, 0:56], reads=[ccdst_s_r], writes=[gat_r])
            for j in range(4):
                for m in range(4):
                    sc = exs[:, 4 + 4 * j + m:5 + 4 * j + m]
                    if m == 0:
                        ts("dve", coef[:, j, :], gat[:, m, 0:44], sc, None, ALU.mult, None, [gat_r, const_r], [coef_r])
                    else:
                        stt(coef[:, j, :], gat[:, m, 0:44], sc, coef[:, j, :], ALU.mult, ALU.add, [gat_r, coef_r, const_r], [coef_r])
            act(coef[:, :, :], coef[:, :, :], AF.Exp, [coef_r], [coef_r])
            for j in range(4):
                ts("dve", coef[:, j, :], coef[:, j, :], exs[:, j:j + 1], None, ALU.mult, None, [coef_r, const_r], [coef_r])
            for j in range(4):
                if j == 0:
                    tt("dve", hin[:, :], coef[:, j, 32:44], gat[:, j, 44:56], ALU.mult, [coef_r, gat_r], [hin_r])
                else:
                    tt("dve", lsum[:, 0:12], coef[:, j, 32:44], gat[:, j, 44:56], ALU.mult, [coef_r, gat_r], [lsum_r])
                    tt("dve", hin[:, :], hin[:, :], lsum[:, 0:12], ALU.add, [hin_r, lsum_r], [hin_r])

        if "lru2" not in STOP:
          with Phase("lru2") as ph:
            laL = [ph.tile(f"laL{i}", [128, T], BF16) for i in range(2)]
            uL = [ph.tile(f"uL{i}", [128, T], BF16) for i in range(2)]
            sg = [ph.tile(f"sg{i}", [128, T], F32) for i in range(2)]
            aa = [ph.tile(f"aa{i}", [128, T], F32) for i in range(2)]
            hh = [ph.tile(f"hh{i}", [128, T], F32) for i in range(2)]
            yb = [ph.tile(f"yb{i}", [128, T], BF16) for i in range(2)]
            l2w = {0: win_tile(COL_LG, 128)}
            for j in range(12):
                la_t, la_r = laL[j % 2]
                u_t, u_r = uL[j % 2]
                P.dma("sp", la_t[:, :], la_d[128 * j:128 * j + 128, :], reads=[la_dr[j]], writes=[la_r])
                P.dma("sp", u_t[:, :], u_d[128 * j:128 * j + 128, :], reads=[u_dr[j]], writes=[u_r])
                if j + 1 < 12:
                    l2w[j + 1] = win_tile(COL_LG + 128 * (j + 1), 128)
                wv, wr = l2w.pop(j)
                s_t, s_r = sg[j % 2]
                a_t, a_r = aa[j % 2]
                h_t, h_r = hh[j % 2]
                y_t, y_r = yb[j % 2]
                for hf in range(2):
                    tk = slice(1024 * hf, 1024 * hf + 1024)
                    pr, prr = inproj_half(wv, wr, 128, hf)
                    act(s_t[:, tk], pr[:, :], AF.Silu, [prr], [s_r])
                for hf in range(2):
                    tk = slice(1024 * hf, 1024 * hf + 1024)
                    act(a_t[:, tk], la_t[:, tk], AF.Exp, [la_r], [a_r])
                for hf in range(2):
                    tk = slice(1024 * hf, 1024 * hf + 1024)
                    init = hin[:, j:j + 1] if hf == 0 else h_t[:, 1023:1024]
                    P.op("dve", (lambda tk=tk, init=init, h_t=h_t, a_t=a_t, u_t=u_t: (lambda e: e.tensor_tensor_scan(out=h_t[:, tk], data0=a_t[:, tk], data1=u_t[:, tk], initial=init, op0=ALU.mult, op1=ALU.add)))(),
                         [a_r, u_r, h_r, hin_r], [h_r])
                    tt("pool", y_t[:, tk], h_t[:, tk], s_t[:, tk], ALU.mult, [h_r, s_r], [y_r])
                P.dma("pool", ylru_d[128 * j:128 * j + 128, :], y_t[:, :], reads=[y_r], writes=[ylru_dr[j]])

        def load_cast(dst, src, nparts, ncols, res):
            for c0 in range(0, ncols, 1024):
                n = min(1024, ncols - c0)
                s32, r32 = wst.next()
                P.dma("sp", s32[0:nparts, 0:n], src[:, c0:c0 + n], writes=[r32])
                copy("pool", dst[0:nparts, c0:c0 + n], s32[0:nparts, 0:n], [r32], [res])

        if "ssd2" not in STOP:
          with Phase("ssd2") as ph:
            selb, selb_r = ph.tile("selb", [64, 32 * 128], BF16)
            dib, dib_r = ph.tile("dib", [128, 32 * 128], BF16)
            xt, xt_r = ph.tile("xt", [128, NCH, 512], BF16)
            bt, bt_r = ph.tile("bt", [128, NCH, 128], BF16)
            bTt, bTt_r = ph.tile("bTt", [128, T], BF16)
            cTt, cTt_r = ph.tile("cTt", [128, T], BF16)
            S, S_r = ph.tile("S", [128, 512], F32)
            Sd, Sd_r = ph.tile("Sd", [128, 512], F32)
            Sbs = [ph.tile(f"Sb{i}", [128, 512], BF16) for i in range(2)]
            Sj = [ph.tile(f"Sj{i}", [128, 512], F32) for i in range(2)]
            Stmp, Stmp_r = ph.tile("Stmp", [128, 512], F32)
            CBm = [ph.tile(f"CBm{i}", [128, 128], BF16) for i in range(3)]
            Esb = [ph.tile(f"Esb{i}", [128, 512], BF16) for i in range(4)]
            Mp = [ph.tile(f"Mp{i}", [128, 4, 128], BF16) for i in range(4)]
            EAsb = [ph.tile(f"EAsb{i}", [128, 512], BF16) for i in range(4)]
            CE = [ph.tile(f"CE{i}", [128, 4, 128], BF16) for i in range(4)]
            xw2 = [ph.tile(f"xw2{i}", [128, 512], BF16) for i in range(3)]
            Rt = [ph.tile(f"Rt{i}", [64, 4, 128], BF16) for i in range(6)]
            neg4, neg4_r = ph.tile("neg4", [128, 4, 128], BF16)
            for k4 in range(4):
                copy("pool", neg4[:, k4, :], neg_b, [const_r], [neg4_r])
            sz4, sz4_r = ph.tile("sz4", [128, 4, T], BF16)
            yz4, yz4_r = ph.tile("yz4", [128, 4, T], BF16)
            rstdg, rstdg_r = ph.tile("rstdg", [128, T], F32)
            yo = [ph.tile(f"yo{i}", [128, T], BF16) for i in range(1)]
            sqz = yo
            load_cast(selb, sel_d, 64, 32 * 128, selb_r)
            load_cast(dib, di_d, 128, 32 * 128, dib_r)
            cc_rows = cc_dst.ap()
            ybank, yb_r = pair[2][:, 0:512], pair_r[2][0]
            stp, stp_r = pair[2][:, 512:1024], pair_r[2][1]
            v8 = lambda t: t.rearrange("p (h q) -> p h q", h=8)
            def zproj(g):
                for i in range(4):
                    wv, wr = win_tile(COL_Z + 512 * g + 128 * i, 128)
                    for hf in range(2):
                        pr, prr = inproj_half(wv, wr, 128, hf)
                        act(sz4[:, i, 1024 * hf:1024 * hf + 1024], pr[:, :], AF.Silu, [prr], [sz4_r])

            zproj(0)
            for g in range(4):
                P.dma("sp", xt[:, :, :].rearrange("p c n -> p (c n)"), xtm_d[g], reads=[grp_dr[g]], writes=[xt_r])
                P.dma("sp", bt[:, :, :].rearrange("p c n -> p (c n)"), btm_d[g], reads=[grp_dr[g]], writes=[bt_r])
                P.dma("sp", bTt[:, :], bT_d[g], reads=[grp_dr[g]], writes=[bTt_r])
                P.dma("sp", cTt[:, :], cT_d[g], reads=[grp_dr[g]], writes=[cTt_r])
                for j in range(4):
                    sj, sj_r = Sj[j % 2]
                    P.dma("sp", sj[:, :], cc_rows[128 * j:128 * j + 128, 512 * g:512 * g + 512], reads=[ccdst_r], writes=[sj_r])
                    cb = coef[:, j, 8 * g:8 * g + 8].unsqueeze(2).to_broadcast([128, 8, 64])
                    if j == 0:
                        tt("dve", v8(S[:, :]), v8(sj[:, :]), cb, ALU.mult, [sj_r, coef_r], [S_r])
                    else:
                        tt("dve", v8(Stmp[:, :]), v8(sj[:, :]), cb, ALU.mult, [sj_r, coef_r], [Stmp_r])
                        tt("dve", S[:, :], S[:, :], Stmp[:, :], ALU.add, [S_r, Stmp_r], [S_r])
                sb0, sb0_r = Sbs[0]
                copy("dve", sb0[:, :], S[:, :], [S_r], [sb0_r])

                def EXa(c):
                    ck = slice(128 * c, 128 * c + 128)
                    P.op("pe", (lambda ck=ck: (lambda e: e.matmul(pm[:, 0:128], lhsT=bTt[:, ck], rhs=cTt[:, ck], start=True, stop=True)))(), [bTt_r, cTt_r], [pm_r])
                    cbm, cbm_r = CBm[c % 3]
                    outs = []
                    tt("dve", cbm[:, :], pm[:, 0:128], U_f, ALU.mult, [pm_r, const_r], [cbm_r])
                    for qb in range(2):
                        k = 2 * c + qb
                        et, et_r = pair[0][:, 512 * qb:512 * qb + 512], pair_r[0][qb]
                        xtile, xtile_r = pair[1][:, 512 * qb:512 * qb + 512], pair_r[1][qb]
                        h0 = 8 * g + 4 * qb
                        selblk = selb[0:64, 128 * h0:128 * h0 + 512]
                        rt_, rt_r = Rt[k % 6]
                        rflat = rt_[:, :, :].rearrange("p a l -> p (a l)")
                        mm(et, [(ident_b, neg4[:, :, :].rearrange("p a l -> p (a l)")), (ones_b[0:64, :], rflat), (rowS[0:64, ck], selblk)],
                           [selb_r, hl_r, const_r, neg4_r, rt_r], [et_r])
                        mm(xtile, [(ones_b[0:64, :], rflat)], [rt_r, const_r], [xtile_r])
                        es_, es_r = Esb[k % 4]
                        ea_, ea_r = EAsb[k % 4]
                        mp_, mp_r = Mp[k % 4]
                        ce_, ce_r = CE[k % 4]
                        act(es_[:, :], et, AF.Exp, [et_r], [es_r])
                        act(ea_[:, :], xtile, AF.Exp, [xtile_r], [ea_r])
                        outs.append((mp_, mp_r, ce_, ce_r, es_, es_r, ea_, ea_r, cbm, cbm_r, ck))
                    return outs

                def RT(c):
                    ck = slice(128 * c, 128 * c + 128)
                    for qb in range(2):
                        k = 2 * c + qb
                        h0 = 8 * g + 4 * qb
                        selblk = selb[0:64, 128 * h0:128 * h0 + 512]
                        rt_, rt_r = Rt[k % 6]
                        tt("dve", rt_[:, :, :], selblk.rearrange("p (a l) -> p a l", a=4), acsS[0:64, ck].unsqueeze(1).to_broadcast([64, 4, 128]), ALU.mult, [selb_r, hl_r], [rt_r])

                def EXb(outs):
                    res = []
                    for (mp_, mp_r, ce_, ce_r, es_, es_r, ea_, ea_r, cbm, cbm_r, ck) in outs:
                        tt("dve", mp_[:, :, :], es_[:, :].rearrange("p (a l) -> p a l", a=4), cbm[:, :].unsqueeze(1).to_broadcast([128, 4, 128]), ALU.mult, [es_r, cbm_r], [mp_r])
                        tt("pool", ce_[:, :, :], ea_[:, :].rearrange("p (a l) -> p a l", a=4), cTt[:, ck].unsqueeze(1).to_broadcast([128, 4, 128]), ALU.mult, [ea_r, cTt_r], [ce_r])
                        res.append((mp_, mp_r, ce_, ce_r))
                    return res

                def XW(c):
                    xw, xw_r = xw2[c % 3]
                    tt("pool", v8(xw[:, :]), v8(xt[:, c, :]),
                       w2s[:, 32 * c + 8 * g:32 * c + 8 * g + 8].unsqueeze(2).to_broadcast([128, 8, 64]), ALU.mult, [xt_r, ssm_r], [xw_r])

                def ST(c):
                    xw, xw_r = xw2[c % 3]
                    mm(stp, [(bt[:, c, :], xw[:, :])], [bt_r, xw_r], [stp_r])

                XW(0)
                RT(0)
                RT(1)
                nxt = EXb(EXa(0))
                XW(1)
                tt("dve", v8(Sd[:, :]), v8(S[:, :]), decs[:, 8 * g:8 * g + 8].unsqueeze(2).to_broadcast([128, 8, 64]), ALU.mult, [S_r, ssm_r], [Sd_r])
                for c in range(NCH):
                    ck = slice(128 * c, 128 * c + 128)
                    cur = nxt
                    if c + 1 < NCH:
                        ST(c)
                        sbn, sbn_r = Sbs[(c + 1) % 2]
                        tt("dve", sbn[:, :], Sd[:, :], stp, ALU.add, [Sd_r, stp_r], [sbn_r])
                        tt("dve", S[:, :], Sd[:, :], stp, ALU.add, [Sd_r, stp_r], [S_r])
                        tt("pool", v8(Sd[:, :]), v8(S[:, :]), decs[:, 32 * (c + 1) + 8 * g:32 * (c + 1) + 8 * g + 8].unsqueeze(2).to_broadcast([128, 8, 64]), ALU.mult, [S_r, ssm_r], [Sd_r])
                        if c + 2 < NCH:
                            RT(c + 2)
                            if c + 3 < NCH:
                                XW(c + 2)
                        nxt = EXb(EXa(c + 1))
                    sbc, sbc_r = Sbs[c % 2]
                    for qb in range(2):
                        mp_, mp_r, ce_, ce_r = cur[qb]
                        for hh in range(4):
                            hl = 4 * qb + hh
                            h = 8 * g + hl
                            i, e2 = hl // 2, hl % 2
                            xh = xt[:, c, 64 * hl:64 * hl + 64]
                            mm(ybank[64 * e2:64 * e2 + 64, 128 * i:128 * i + 128],
                               [(xh, mp_[:, hh, :]), (sbc[:, 64 * hl:64 * hl + 64], ce_[:, hh, :]), (xh, dib[:, 128 * h:128 * h + 128])],
                               [xt_r, mp_r, sbc_r, ce_r, dib_r], [yb_r])
                    tt("dve", yz4[:, :, ck], ybank.rearrange("p (i l) -> p i l", i=4), sz4[:, :, ck], ALU.mult, [yb_r, sz4_r], [yz4_r])
                if g + 1 < 4:
                    zproj(g + 1)
                for i in range(4):
                    sq, sq_r = sqz[0]
                    act(sq[:, :], yz4[:, i, :], AF.Square, [yz4_r], [sq_r])
                    for blk in range(4):
                        o = pair[blk // 2][:, 512 * (blk % 2):512 * (blk % 2) + 512]
                        P.op("pe", (lambda o=o, sq=sq, blk=blk, i=i: (lambda e: e.matmul(o, lhsT=ones_b, rhs=sq[:, 512 * blk:512 * blk + 512], start=(i == 0), stop=(i == 3))))(),
                             [sq_r, const_r], [pair_r[blk // 2][blk % 2]])
                for pi in range(2):
                    ts("dve", rstdg[:, 1024 * pi:1024 * pi + 1024], pair[pi][:, :], 1.0 / 512.0, EPS, ALU.mult, ALU.add, [pair_r[pi]], [rstdg_r])
                act(rstdg[:, :], rstdg[:, :], AF.Ln, [rstdg_r], [rstdg_r])
                act(rstdg[:, :], rstdg[:, :], AF.Exp, [rstdg_r], [rstdg_r], scale=-0.5)
                for i in range(4):
                    y_, y_r = yo[0]
                    stt(y_[:, :], yz4[:, i, :], pp[:, PP_SNG + 4 * g + i:PP_SNG + 4 * g + i + 1], rstdg[:, :], ALU.mult, ALU.mult, [yz4_r, rstdg_r, const_r], [y_r])
                    kt = 4 * g + i
                    P.dma("pool", yssd_d[128 * kt:128 * kt + 128, :], y_[:, :], reads=[y_r], writes=[yssd_dr[kt]])

        if "merge" not in STOP:
          with Phase("merge") as ph:
            ys, ys_r = ph.tile("ys", [128, 16, 1024], BF16)
            yl, yl_r = ph.tile("yl", [128, 12, 1024], BF16)
            ym, ym_r = ph.tile("ym", [128, 8, 1024], BF16)
            mT, mT_r = ph.tile("mT", [128, 8, 1024], BF16)
            gs = [ph.tile(f"gs{i}", [128, 1024], F32) for i in range(3)]
            macc, macc_r = ph.tile("macc", [128, 1024], F32)
            mtmp, mtmp_r = ph.tile("mtmp", [128, 1024], F32)
            xtk = [ph.tile(f"xtk{i}", [128, 1024], F32) for i in range(2)]
            rs = [ph.tile(f"rs{i}", [128, 1024], F32) for i in range(2)]
            ssq, ssq_r = ph.tile("ssq", [128, 16], F32)
            ot = [ph.tile(f"ot{i}", [128, 1024], F32) for i in range(1)]
            w_out_v = w_out_d.rearrange("(j p) n -> p j n", p=128)
            ys_v = yssd_d.rearrange("(kt p) t -> p kt t", p=128)
            yl_v = ylru_d.rearrange("(kt p) t -> p kt t", p=128)
            ym_v = ymem_d.rearrange("(kt p) t -> p kt t", p=128)
            w_bs_v = w_bs_d.rearrange("(kt p) n -> p kt n", p=128)
            w_bl_v = w_bl_d.rearrange("(kt p) n -> p kt n", p=128)
            w_bm_v = w_bm_d.rearrange("(kt p) n -> p kt n", p=128)
            P.op("dve", lambda e: e.memset(ssq[:, :], 0.0), [], [ssq_r])
            ys_rk = [Res() for _ in range(16)]
            yl_rk = [Res() for _ in range(12)]
            ym_rk = [Res() for _ in range(8)]
            ssq_rk = [Res() for _ in range(16)]

            def load_lm(tb):
                tsl = slice(1024 * tb, 1024 * tb + 1024)
                for kt in range(12):
                    P.dma("sp", yl[:, kt, :], yl_v[:, kt, tsl], reads=[ylru_dr[kt]], writes=[yl_rk[kt]])
                for kt in range(8):
                    P.dma("sp", ym[:, kt, :], ym_v[:, kt, tsl], reads=[ymem_dr[kt]], writes=[ym_rk[kt]])

            def load_s(tb):
                tsl = slice(1024 * tb, 1024 * tb + 1024)
                for kt in range(16):
                    P.dma("sp", ys[:, kt, :], ys_v[:, kt, tsl], reads=[yssd_dr[kt]], writes=[ys_rk[kt]])

            load_lm(0)
            load_s(0)
            for tb in range(2):
                tsl = slice(1024 * tb, 1024 * tb + 1024)
                for j in range(8):
                    branches = [(ys, ys_rk, w_bs_v, 16), (yl, yl_rk, w_bl_v, 12), (ym, ym_rk, w_bm_v, 8)]
                    gl = []
                    for b in range(3):
                        gw, gwr = win_tile(COL_G + 1024 * b + 128 * j, 128)
                        g_, g_r = gs[b]
                        pr, prr = inproj_half(gw, gwr, 128, tb)
                        act(g_[:, :], pr[:, :], AF.Sigmoid, [prr], [g_r])
                        gl.append((g_, g_r))
                    for b, (yy, yy_rk, wvw, nk) in enumerate(branches):
                        g_, g_r = gl[b]
                        prp, prp_r = next_pair()
                        k0 = 0
                        first = True
                        while k0 < nk:
                            n = min(8, nk - k0)
                            wv, wr = wload([(0, n, wvw[:, k0:k0 + n, 128 * j:128 * j + 128])], n, 128)
                            last = (k0 + n >= nk)
                            for blk in range(2):
                                mm(prp[:, 512 * blk:512 * blk + 512], [(wv[:, kk, :], yy[:, k0 + kk, 512 * blk:512 * blk + 512]) for kk in range(n)],
                                   [wr] + yy_rk[k0:k0 + n], [prp_r[blk]], start=first, stop=last)
                            first = False
                            k0 += n
                        if b == 0:
                            tt("dve", macc[:, :], prp[:, :], g_[:, :], ALU.mult, [prp_r, g_r], [macc_r])
                        else:
                            tt("dve", mtmp[:, :], prp[:, :], g_[:, :], ALU.mult, [prp_r, g_r], [mtmp_r])
                            if b == 1:
                                tt("pool", macc[:, :], macc[:, :], mtmp[:, :], ALU.add, [macc_r, mtmp_r], [macc_r])
                            else:
                                tt("pool", mT[:, j, :], macc[:, :], mtmp[:, :], ALU.add, [macc_r, mtmp_r], [mT_r])
                wo, wo_r = ys, ys_rk[0:8]
                for j in range(8):
                    s32, r32 = wst.next()
                    P.dma("sp", s32[:, 0:1024], w_out_v[:, j, :], writes=[r32])
                    copy("dve", wo[:, j, :], s32[:, 0:1024], [r32], [ys_rk[j]])
                if tb == 0:
                    load_lm(1)
                for t8 in range(8):
                    k = 8 * tb + t8
                    xk, xk_r = xtk[k % 2]
                    r_, r_r = rs[k % 2]
                    o_, o_r = ot[0]
                    P.dma("sp", xk[:, :], xtok_d[128 * k:128 * k + 128, :], writes=[xk_r])
                    pr, prr = next_pair()
                    for half in range(2):
                        mm(pr[:, 512 * half:512 * half + 512], [(mT[:, j, 128 * t8:128 * t8 + 128], wo[:, j, 512 * half:512 * half + 512]) for j in range(8)], [mT_r, wo_r], [prr])
                    tt("dve", r_[:, :], pr[:, :], xk[:, :], ALU.add, [prr, xk_r], [r_r])
                    kr = ssq_rk[k]
                    act(o_[:, :], r_[:, :], AF.Square, [r_r, ssq_r, kr], [o_r, kr], accum=ssq[:, k:k + 1])
                    ts("dve", ssq[:, k:k + 1], ssq[:, k:k + 1], 1.0 / D, EPS, ALU.mult, ALU.add, [kr], [kr])
                    act(ssq[:, k:k + 1], ssq[:, k:k + 1], AF.Sqrt, [kr], [kr])
                    P.op("dve", (lambda k=k: (lambda e: e.reciprocal(out=ssq[:, k:k + 1], in_=ssq[:, k:k + 1])))(), [kr], [kr])
                    stt(o_[:, :], r_[:, :], ssq[:, k:k + 1], pb[:, PB_FG:PB_FG + 1024], ALU.mult, ALU.mult, [r_r, kr, const_r], [o_r])
                    final_ops.append(P.dma("pool", out_d[128 * k:128 * k + 128, :], o_[:, :], reads=[o_r]))
                if tb == 0:
                    load_s(1)

        if DEBUG:
            final_ops.append(P.dma("pool", dbg_d[:, 0:64], pay[:, :], reads=[pay_r]))
            final_ops.append(P.dma("pool", dbg_d[:, 64:64 + 2048], hT[:, 0, 3:TH], reads=[hT_r]))
            final_ops.append(P.dma("pool", dbg_d[:, 2112:2112 + 512], w2s[:, :], reads=[ssm_r]))
            final_ops.append(P.dma("pool", dbg_d[:, 2624:2624 + 512], decs[:, :], reads=[ssm_r]))
            final_ops.append(P.dma("pool", dbg_d[:, 3136:3136 + 12], hin[:, :], reads=[hin_r]))
            final_ops.append(P.dma("pool", dbg_d[:, 3200:3200 + 176], coef[:, :, :].rearrange("p a b -> p (a b)"), reads=[coef_r]))
            final_ops.append(P.dma("pool", dbg_d[0:64, 4096:4096 + 2048], acsS[:, :], reads=[hl_r])) if False else None
        fr = Res("final")
        for o in final_ops:
            if o is not None:
                fr.r.append(o)
        P.op("sp", lambda e: None, [], [fr])

        P.emit(nc, block, sems, dsem)
    return nc


def _tile_pp(v):
    return np.ascontiguousarray(v.reshape(-1, 128).T)


def prep_inputs(inp):
    f = np.float32
    x, mem = inp["x"], inp["mem"]
    pp = np.zeros((128, PP_N), f)
    pp[:, PP_NORMG:PP_NORMG + 8] = _tile_pp(inp["norm_g"][0])
    pp[:, PP_MEMG:PP_MEMG + 8] = _tile_pp(inp["mem_norm_g"][0])
    scw, scb = inp["ssd_conv_w"][0], inp["ssd_conv_b"][0]
    for t in range(24):
        for k in range(4):
            pp[:, PP_SCONV + 5 * t + k] = scw[k, t * 128:(t + 1) * 128]
        pp[:, PP_SCONV + 5 * t + 4] = scb[t * 128:(t + 1) * 128]
    lcw, lcb = inp["lru_conv_w"][0], inp["lru_conv_b"][0]
    for t in range(12):
        for k in range(4):
            pp[:, PP_LCONV + 5 * t + k] = lcw[k, t * 128:(t + 1) * 128]
        pp[:, PP_LCONV + 5 * t + 4] = lcb[t * 128:(t + 1) * 128]
    pp[:, PP_LBA:PP_LBA + 12] = _tile_pp(inp["lru_b_a"][0].reshape(-1))
    pp[:, PP_LBX:PP_LBX + 12] = _tile_pp(inp["lru_b_x"][0].reshape(-1))
    pp[:, PP_LAM:PP_LAM + 12] = _tile_pp(inp["lru_lambda"][0])
    pp[:, PP_SNG:PP_SNG + 16] = _tile_pp(inp["ssd_norm_g"][0].reshape(-1))
    pp[0:32, PP_DTB] = inp["ssd_dt_bias"][0]
    pp[32:64, PP_DTB] = inp["ssd_dt_bias"][0]
    pp[0:32, PP_ALOG] = inp["ssd_a_log"][0]
    pb = np.zeros((128, PB_N), f)
    pb[:, PB_DTB:PB_DTB + 32] = inp["ssd_dt_bias"][0][None, :]
    pb[:, PB_ALOG:PB_ALOG + 32] = inp["ssd_a_log"][0][None, :]
    pb[:, PB_FG:PB_FG + 1024] = inp["final_g"][None, :]
    cst = np.zeros((128, C_N), f)
    cst[:, C_ID:C_ID + 128] = np.eye(128, dtype=f)
    cst[:, C_U:C_U + 128] = np.triu(np.ones((128, 128), f))
    cst[:, C_ONE:C_ONE + 128] = 1.0
    cst[:, C_NEG:C_NEG + 128] = np.tril(np.full((128, 128), -32768.0, f), -1)
    sel2 = np.zeros((64, 32, 128), f)
    for h in range(32):
        sel2[h, h, :] = 1.0
        sel2[32 + h, h, :] = 1.0
    sel2 = sel2.reshape(64, 32 * 128)
    dih = np.zeros((128, 32, 128), f)
    dd = inp["ssd_d"][0]
    for h in range(32):
        dih[np.arange(128), h, np.arange(128)] = dd[h]
    dih = dih.reshape(128, 32 * 128)

    def bd(w):
        m = np.zeros((1536, 1536), f)
        for n in range(16):
            m[n * 96:(n + 1) * 96, n * 96:(n + 1) * 96] = w[n]
        return m
    wa_bd, wx_bd = bd(inp["lru_w_a"][0]), bd(inp["lru_w_x"][0])
    shared = {"w_in": np.ascontiguousarray(inp["w_in"][0]), "w_kv": np.ascontiguousarray(inp["w_kv"][0]),
              "w_br_ssd": np.ascontiguousarray(inp["w_br_ssd"][0]), "w_br_lru": np.ascontiguousarray(inp["w_br_lru"][0]),
              "w_br_mem": np.ascontiguousarray(inp["w_br_mem"][0]), "w_out": np.ascontiguousarray(inp["w_out"][0]),
              "lru_wa_bd": wa_bd, "lru_wx_bd": wx_bd, "pp": pp, "pb": pb, "cst": cst, "sel2": sel2, "dih": dih}
    maps = []
    for c in range(8):
        b, q = c // 4, c % 4
        t0 = q * T
        xs = np.zeros((TH, D), f)
        if q == 0:
            xs[3:] = x[b, 0:T]
        else:
            xs[:] = x[b, t0 - 3:t0 + T]
        xT = np.ascontiguousarray(xs.T.reshape(8, 128, TH).transpose(1, 0, 2))
        memT = np.ascontiguousarray(mem[b].T.reshape(8, 128, 256).transpose(1, 0, 2))
        exsel = np.zeros((128, 20), f)
        for j in range(4):
            exsel[:, j] = 1.0 if j < q else 0.0
            for m in range(4):
                exsel[:, 4 + 4 * j + m] = 1.0 if (j < m < q) else 0.0
        d = dict(shared)
        d.update({"xT": xT, "x_tok": np.ascontiguousarray(x[b, t0:t0 + T]), "memT": memT, "exsel": exsel})
        maps.append(d)
    return maps


_NC = None


def kernel(**inputs):
    global _NC
    inp = {k: np.asarray(v) for k, v in inputs.items()}
    maps = prep_inputs(inp)
    if _NC is None:
        _NC = build_nc()
    res = run_bass_kernel_spmd(_NC, maps, core_ids=list(range(8)))
    out = np.zeros((2, 4 * T, D), np.float32)
    for c in range(8):
        out[c // 4, (c % 4) * T:(c % 4 + 1) * T] = res.results[c]["out"]
    kernel.last = res
    return out
```
